# Optimizing a Trainium2 kernel written in Bass

```python
import math
import jax, jax.numpy as jnp
from jax import lax
import numpy as np

D_MODEL = 1024
BATCH = 4
SEQ = 4096
DEPTH = 1

EXPAND = 2
D_MIX = EXPAND * D_MODEL
W_CONF = D_MIX // 2
W_HYENA = D_MIX - W_CONF
N_GROUPS_CONF = 8
N_GROUPS_HYENA = 8
CONF_KERNEL = 31
HYENA_ORDER = 2
HYENA_SHORT_KERNEL = 3
FILTER_EMB_DIM = 33
FILTER_ORDER = 64
FILTER_SIN_W = 1.0
DECAY_TARGET = 1e-2
FAST_DECAY_PCT = 0.3
SLOW_DECAY_PCT = 1.5
NORM_EPS = 1e-5
D_IN_PROJ = 3 * W_CONF + (HYENA_ORDER + 1) * W_HYENA + W_HYENA
SPLIT_IDX = [W_CONF, 2 * W_CONF, 3 * W_CONF, 3 * W_CONF + (HYENA_ORDER + 1) * W_HYENA]

kernel_name = "hybrid_conformer_hyena_block"


def rmsnorm(x, g):
    xf = x.astype(jnp.float32)
    y = xf * lax.rsqrt(jnp.mean(xf * xf, axis=-1, keepdims=True) + NORM_EPS)
    return (y * g.astype(jnp.float32)).astype(x.dtype)


def group_layernorm(x, g, b, n_groups):
    B, L, C = x.shape
    xf = x.astype(jnp.float32).reshape(B, L, n_groups, C // n_groups)
    mu = jnp.mean(xf, axis=-1, keepdims=True)
    var = jnp.mean(jnp.square(xf - mu), axis=-1, keepdims=True)
    y = ((xf - mu) * lax.rsqrt(var + NORM_EPS)).reshape(B, L, C)
    return (y * g.astype(jnp.float32) + b.astype(jnp.float32)).astype(x.dtype)


def group_rmsnorm(x, g, n_groups):
    B, L, C = x.shape
    xf = x.astype(jnp.float32).reshape(B, L, n_groups, C // n_groups)
    y = (xf * lax.rsqrt(jnp.mean(xf * xf, axis=-1, keepdims=True) + NORM_EPS)).reshape(B, L, C)
    return (y * g.astype(jnp.float32)).astype(x.dtype)


def depthwise_conv_centred(x, w, b):
    K, C = w.shape
    pad = K // 2
    y = lax.conv_general_dilated(
        x, w[:, None, :].astype(x.dtype), window_strides=(1,),
        padding=((pad, pad),), dimension_numbers=('NWC', 'WIO', 'NWC'),
        feature_group_count=C)
    return y + b.astype(x.dtype)


def hyena_pos_features(L):
    t = jnp.linspace(0.0, 1.0, L, dtype=jnp.float32)[:, None]
    bands = (FILTER_EMB_DIM - 1) // 2
    w = (2.0 * math.pi / L) * jnp.arange(L, dtype=jnp.float32)[:, None]
    f = jnp.linspace(1e-4, bands - 1, bands, dtype=jnp.float32)[None, :]
    z = jnp.concatenate([t, jnp.cos(f * w), -jnp.sin(f * w)], axis=-1)
    return t, z


def hyena_filters(t, z, w1, b1, fr1, w2, b2, fr2, w3, b3, fr3, w_out, deltas):
    f32 = jnp.float32
    h = jnp.sin(fr1.astype(f32) * (z @ w1.astype(f32) + b1.astype(f32)))
    h = jnp.sin(fr2.astype(f32) * (h @ w2.astype(f32) + b2.astype(f32)))
    h = jnp.sin(fr3.astype(f32) * (h @ w3.astype(f32) + b3.astype(f32)))
    h = (h @ w_out.astype(f32)).reshape(z.shape[0], 2, W_HYENA)
    decay = jnp.exp(-t[:, :, None] * jnp.abs(deltas.astype(f32))[None])
    h = h * decay
    return h[:, 0], h[:, 1]


def bidir_fftconv(v, h_fwd, h_bwd, skip):
    B, L, C = v.shape
    n = 2 * L
    k = jnp.concatenate([
        h_fwd.at[0].add(h_bwd[0]),
        jnp.zeros((1, C), jnp.float32),
        h_bwd[1:][::-1],
    ], axis=0)
    vf32 = v.astype(jnp.float32)
    vf = jnp.fft.rfft(vf32, n=n, axis=1)
    kf = jnp.fft.rfft(k, n=n, axis=0)
    y = jnp.fft.irfft(vf * kf[None], n=n, axis=1)[:, :L]
    y = y + vf32 * skip.astype(jnp.float32)
    return y.astype(v.dtype)


def setup_inputs(seed: int = 0) -> dict:
    key = jax.random.key(seed)
    ks = jax.random.split(key, 32)
    f32 = jnp.float32
    nrm = lambda k, s, sc: (jax.random.normal(k, s, f32) * sc)
    base = jnp.linspace(math.log(DECAY_TARGET) / SLOW_DECAY_PCT,
                        math.log(DECAY_TARGET) / FAST_DECAY_PCT, W_HYENA, dtype=f32)
    deltas = jnp.stack([base, base[::-1]])[None] * (1.0 + nrm(ks[20], (DEPTH, 2, W_HYENA), 0.05))
    return {
        "x": nrm(ks[0], (BATCH, SEQ, D_MODEL), 1.0),
        "norm_g": 1.0 + nrm(ks[1], (DEPTH, D_MODEL), 0.02),
        "w_in": nrm(ks[2], (DEPTH, D_MODEL, D_IN_PROJ), D_MODEL ** -0.5),
        "conf_dw_w": nrm(ks[3], (DEPTH, CONF_KERNEL, W_CONF), CONF_KERNEL ** -0.5),
        "conf_dw_b": nrm(ks[4], (DEPTH, W_CONF), 0.02),
        "conf_ln_g": 1.0 + nrm(ks[5], (DEPTH, W_CONF), 0.02),
        "conf_ln_b": nrm(ks[6], (DEPTH, W_CONF), 0.02),
        "hy_short_w": nrm(ks[7], (DEPTH, HYENA_SHORT_KERNEL, (HYENA_ORDER + 1) * W_HYENA), HYENA_SHORT_KERNEL ** -0.5),
        "hy_short_b": nrm(ks[8], (DEPTH, (HYENA_ORDER + 1) * W_HYENA), 0.02),
        "filt_w1": nrm(ks[9], (DEPTH, FILTER_EMB_DIM, FILTER_ORDER), FILTER_EMB_DIM ** -0.5),
        "filt_b1": nrm(ks[10], (DEPTH, FILTER_ORDER), 0.02),
        "filt_freq1": FILTER_SIN_W + nrm(ks[11], (DEPTH, FILTER_ORDER), 0.01),
        "filt_w2": nrm(ks[12], (DEPTH, FILTER_ORDER, FILTER_ORDER), FILTER_ORDER ** -0.5),
        "filt_b2": nrm(ks[13], (DEPTH, FILTER_ORDER), 0.02),
        "filt_freq2": FILTER_SIN_W + nrm(ks[14], (DEPTH, FILTER_ORDER), 0.01),
        "filt_w3": nrm(ks[15], (DEPTH, FILTER_ORDER, FILTER_ORDER), FILTER_ORDER ** -0.5),
        "filt_b3": nrm(ks[16], (DEPTH, FILTER_ORDER), 0.02),
        "filt_freq3": FILTER_SIN_W + nrm(ks[17], (DEPTH, FILTER_ORDER), 0.01),
        "filt_w_out": nrm(ks[18], (DEPTH, FILTER_ORDER, 2 * W_HYENA), FILTER_ORDER ** -0.5),
        "hy_deltas": deltas,
        "hy_skip": nrm(ks[21], (DEPTH, W_HYENA), 1.0),
        "hy_norm_g": 1.0 + nrm(ks[22], (DEPTH, W_HYENA), 0.02),
        "w_out": nrm(ks[23], (DEPTH, D_MIX, D_MODEL), D_MIX ** -0.5),
        "final_g": 1.0 + nrm(ks[24], (D_MODEL,), 0.02),
    }


def reference(x, norm_g, w_in, conf_dw_w, conf_dw_b, conf_ln_g, conf_ln_b,
              hy_short_w, hy_short_b, filt_w1, filt_b1, filt_freq1,
              filt_w2, filt_b2, filt_freq2, filt_w3, filt_b3, filt_freq3,
              filt_w_out, hy_deltas, hy_skip, hy_norm_g, w_out, final_g):
    L = x.shape[1]
    t, z = hyena_pos_features(L)
    h = x
    for l in range(DEPTH):
        u = rmsnorm(h, norm_g[l])
        p = jnp.einsum('bld,de->ble', u, w_in[l])
        c_val, c_gate, c_z, hy_p, hy_z = jnp.split(p, SPLIT_IDX, axis=-1)

        a = c_val * jax.nn.sigmoid(c_gate)
        a = depthwise_conv_centred(a, conf_dw_w[l], conf_dw_b[l])
        a = group_layernorm(a, conf_ln_g[l], conf_ln_b[l], N_GROUPS_CONF)
        a = jax.nn.silu(a) * jax.nn.silu(c_z)

        hp = depthwise_conv_centred(hy_p, hy_short_w[l], hy_short_b[l])
        x0, x1, v = jnp.split(hp, HYENA_ORDER + 1, axis=-1)
        h_fwd, h_bwd = hyena_filters(t, z, filt_w1[l], filt_b1[l], filt_freq1[l],
                                     filt_w2[l], filt_b2[l], filt_freq2[l],
                                     filt_w3[l], filt_b3[l], filt_freq3[l],
                                     filt_w_out[l], hy_deltas[l])
        y = bidir_fftconv(v * x1, h_fwd, h_bwd, hy_skip[l]) * x0
        y = group_rmsnorm(y, hy_norm_g[l], N_GROUPS_HYENA) * jax.nn.silu(hy_z)

        mix = jnp.concatenate([a, y], axis=-1)
        h = h + jnp.einsum('blm,md->bld', mix, w_out[l])
    return rmsnorm(h, final_g)
```

```python
import math
from contextlib import ExitStack

import numpy as np
import ml_dtypes

import concourse.bass as bass
import concourse.mybir as mybir
from concourse.ap import AP
from concourse.bass_utils import run_bass_kernel_spmd

F32 = mybir.dt.float32
BF16 = mybir.dt.bfloat16
ALU = mybir.AluOpType
AF = mybir.ActivationFunctionType

L = 4096
D = 1024
EPS = 1e-5
NPV = 384
EVAC_FORCE = None
ATTACH_WAIT = True
MAGIC = 12582912.0
TWO_PI = 2.0 * math.pi


class Sched:
    CHUNK = 3000

    def __init__(self):
        self.ops = []
        self.bufs = {}
        self.known = {}
        self.known_dma = {}
        self.dma_cnt = {}
        self.last_of = {}
        self.psum = set()
        self.dma_kvec = {}

    def add(self, eng, fn, reads=(), writes=(), dma=None):
        oid = len(self.ops)
        is_dma = dma is not None
        need_c = {}
        need_d = {}

        def consider(pid, raw):
            p = self.ops[pid]
            if p['dma'] is not None:
                k = p['dma']
                need_d[k] = max(need_d.get(k, 0), (1 << 30) if k == 'const' else p['dcount'])
                return
            if (not is_dma) and p['eng'] == eng:
                if eng == 'pe':
                    return
            need_c[p['eng']] = max(need_c.get(p['eng'], -1), pid)

        reads = [((n, 0, 1 << 30) if n in self.psum else (n, lo, hi)) for (n, lo, hi) in reads]
        writes = [((n, 0, 1 << 30) if n in self.psum else (n, lo, hi)) for (n, lo, hi) in writes]
        for (n, lo, hi) in reads:
            b = self.bufs.setdefault(n, {'w': [], 'r': []})
            for (l2, h2, pid) in b['w']:
                if l2 < hi and lo < h2:
                    consider(pid, True)
            if n in self.psum:
                for (l2, h2, pid) in b['r']:
                    consider(pid, False)
        for (n, lo, hi) in writes:
            b = self.bufs.setdefault(n, {'w': [], 'r': []})
            for (l2, h2, pid) in b['w']:
                if l2 < hi and lo < h2:
                    consider(pid, False)
            for (l2, h2, pid) in b['r']:
                if l2 < hi and lo < h2:
                    consider(pid, False)
        kn = self.known.setdefault(eng, {})
        kd = self.known_dma.setdefault(eng, {})
        wc = {}
        wd = {}

        def merge(kv):
            for e2, p2 in kv.items():
                if kn.get(e2, -1) < p2:
                    kn[e2] = p2

        for k, c in need_d.items():
            if kd.get(k, 0) < c:
                wd[k] = c
                kd[k] = c
                if k != 'const' and (k, c) in self.dma_kvec:
                    merge(self.dma_kvec[(k, c)])
        for e, pid in sorted(need_c.items(), key=lambda kv: -kv[1]):
            if kn.get(e, -1) < pid:
                wc[e] = pid
                kn[e] = pid
                self.ops[pid]['ms'] = True
                merge(self.ops[pid]['kvec'])
        op = dict(eng=eng, fn=fn, wc=wc, wd=wd, dma=dma, dcount=0, ms=False, kvec=dict(kn))
        if not is_dma:
            op['kvec'][eng] = max(op['kvec'].get(eng, -1), oid - 0)
        if is_dma:
            self.dma_cnt[dma] = self.dma_cnt.get(dma, 0) + 16
            op['dcount'] = self.dma_cnt[dma]
            self.dma_kvec[(dma, op['dcount'])] = dict(kn)
        self.ops.append(op)
        for (n, lo, hi) in writes:
            b = self.bufs[n]
            b['w'] = [r for r in b['w'] if not (lo <= r[0] and r[1] <= hi)]
            b['r'] = [r for r in b['r'] if not (lo <= r[0] and r[1] <= hi)]
            b['w'].append((lo, hi, oid))
        for (n, lo, hi) in reads:
            self.bufs[n]['r'].append((lo, hi, oid))
        return oid

    def emit(self, nc, final_waits):
        engs = ['pe', 'act', 'dve', 'pool', 'sp']
        msn = {}
        cnt = {e: 0 for e in engs}
        for i, op in enumerate(self.ops):
            if op['ms'] and op['dma'] is None:
                msn[i] = cnt[op['eng']]
                cnt[op['eng']] += 1
        with ExitStack() as st:
            csem = {}
            for e in engs:
                n = cnt[e] // self.CHUNK + 1
                csem[e] = [st.enter_context(nc.semaphore("s_%s_%d" % (e, j))) for j in range(n)]
            dsem = {k: st.enter_context(nc.semaphore("d_" + k)) for k in self.dma_cnt}
            block = st.enter_context(nc.Block())

            def run(ename, eobj):
                for i, op in enumerate(self.ops):
                    if op['eng'] != ename:
                        continue
                    waits = []
                    for e, pid in op['wc'].items():
                        m = msn[pid]
                        waits.append((csem[e][m // self.CHUNK], m % self.CHUNK + 1))
                    for k, c in op['wd'].items():
                        waits.append((dsem[k], self.dma_cnt[k] if k == 'const' else c))
                    attach = waits.pop() if (waits and ATTACH_WAIT) else None
                    for (sm_, v_) in waits:
                        eobj.wait_ge(sm_, v_)
                    ins = op['fn'](eobj)
                    if attach is not None:
                        ins._wait_ge(attach[0], attach[1])
                    if op['dma'] is not None:
                        ins.then_inc(dsem[op['dma']], 16)
                    elif op['ms']:
                        m = msn[i]
                        ins.then_inc(csem[ename][m // self.CHUNK], 1)
                if ename == 'sp':
                    for k in final_waits:
                        eobj.wait_ge(dsem[k], self.dma_cnt[k])

            @block.tensor
            def _(e):
                run('pe', e)

            @block.scalar
            def _(e):
                run('act', e)

            @block.vector
            def _(e):
                run('dve', e)

            @block.gpsimd
            def _(e):
                run('pool', e)

            @block.sync
            def _(e):
                run('sp', e)


class TV:
    def __init__(self, h, F, esz, base, coff=0, ncol=None):
        self.h = h
        self.F = F
        self.esz = esz
        self.base = base
        self.coff = coff
        self.ncol = F if ncol is None else ncol

    def ap(self, c0, dims, p0=0, npart=128):
        return AP(self.h, p0 * self.F + self.coff + c0, [[self.F, npart]] + [list(d) for d in dims])

    def r(self, c0=0, c1=None):
        if c1 is None:
            c1 = self.ncol
        return (self.base, (self.coff + c0) * self.esz, (self.coff + c1) * self.esz)

    def cast(self, dt, esz2):
        v = self.h[:, :].bitcast(dt)
        return TV(v.tensor, self.F * self.esz // esz2, esz2, self.base,
                  self.coff * self.esz // esz2, self.ncol * self.esz // esz2)

    def sub(self, c0, n):
        return TV(self.h, self.F, self.esz, self.base, self.coff + c0, n)


def build_program(PH='FXCH3', ncc=8, nch=8, dbg=False, hstop=99):
    nc = bass.Bass("TRN2", target_bir_lowering=False)
    S = Sched()

    def dram(name, shape, dt, kind="ExternalInput"):
        return nc.dram_tensor(name, shape, dt, kind=kind).ap()

    x_d = dram("x", [L, D], F32)
    win_d = dram("w_in", [D, 7168], F32)
    wout_d = dram("w_out", [2048, D], F32)
    pv_d = dram("pv", [128, NPV], F32)
    w1_d = dram("mlpw1", [33, 64], F32)
    w2_d = dram("mlpw2", [64, 64], F32)
    w3_d = dram("mlpw3", [64, 64], F32)
    mp_d = dram("mlpp", [64, 6], F32)
    wf_d = dram("wf", [64, 2048], F32)
    del_d = dram("deltas", [2, 1024], F32)
    skip_d = dram("skip", [1, 1024], F32)
    fg_d = dram("final_g", [1, 1024], F32)
    zT_d = dram("zT", [33, L], F32)
    negt_d = dram("negt", [128, 32], F32)
    M1_d = dram("M1", [128, 8192], BF16)
    S2_d = dram("S2m", [128, 384], BF16)
    R12_d = dram("R12", [128, 512], BF16)
    IM_d = dram("IM", [128, 4096], BF16)
    id_d = dram("ident", [128, 128], BF16)
    out_d = dram("out", [2048, D], F32, kind="ExternalOutput")
    mix_d = dram("mixd", [2048, 2048], BF16, kind=("ExternalOutput" if dbg else "Internal"))
    mix_w = mix_d.rearrange("(tt ml) (mc tok) -> ml tt mc tok", ml=128, tok=128)
    if dbg:
        dbg_hm = dram("dbg_hm", [64, L], BF16, kind="ExternalOutput")
        dbg_xn = dram("dbg_xn", [128, 8 * L], BF16, kind="ExternalOutput")

    with ExitStack() as st:
        def sb(name, F, dt):
            h = st.enter_context(nc.sbuf_tensor(name, [128, F], dt))
            return TV(h, F, 4 if dt == F32 else 2, name)

        def ps(name, F, dt):
            S.psum.add(name)
            h = st.enter_context(nc.psum_tensor(name, [128, F], dt))
            return TV(h, F, 4 if dt == F32 else 2, name)

        xnT = sb("xnT", 8 * L, BF16)
        M1 = sb("M1s", 8192, BF16)
        IM = sb("IMs", 4096, BF16)
        HmT = sb("HmT", L, BF16)
        Wf = sb("Wf", 2048, BF16)
        S2m = sb("S2ms", 384, BF16)
        R12 = sb("R12s", 512, BF16)
        ident = sb("idents", 128, BF16)
        onesm = sb("onesm", 128, F32)
        pv = sb("pvs", NPV, F32)
        negt = sb("negts", 32, F32)
        mlpw = sb("mlpw", 192, F32)
        mlpp = sb("mlpps", 8, F32)
        frb = sb("frb", 4, F32)
        wst = [sb("wst%d" % i, 1024, F32) for i in range(2)]
        wbf = [sb("wbf%d" % i, 1024, BF16) for i in range(2)]
        tmpall = sb("tmpall", 4096, F32)
        tmp = [tmpall.sub(512 * i, 512) for i in range(8)]
        small = [sb("small%d" % i, 8, F32) for i in range(4)]
        XTbuf = sb("XTbuf", 8192, BF16)
        Abuf = sb("Abuf", 4096, F32)
        Abf = Abuf.cast(BF16, 2)
        Kbuf = sb("Kbuf", 8192, BF16)
        raw = [sb("raw%d" % i, 1040, F32) for i in range(2)]
        accbuf = sb("accbuf", 2080, F32)
        a_pad = raw[0].cast(BF16, 2)
        pool4 = [sb("pl%d" % i, 1024, BF16) for i in range(4)]
        absd = sb("absd", 256, F32)
        skipb = sb("skipb", 128, F32)
        xt = [Kbuf.cast(F32, 4).sub(1024 * i, 1024) for i in range(2)]
        xnb = [XTbuf.sub(1024 * i, 1024) for i in range(2)]
        P = [ps("P%d" % i, 512, F32) for i in range(6)]
        Q = [ps("Q%d" % i, 1024, BF16) for i in range(2)]

        cnt = {'p': 0, 'q': 0, 't': 0, 's': 0, 'w': 0, 'pl': 0, 'x': 0, 'e': 0}

        def nextP():
            cnt['p'] += 1
            return P[cnt['p'] % 6]

        def nextQ():
            cnt['q'] += 1
            return Q[cnt['q'] % 2]

        def nextT():
            cnt['t'] += 1
            return tmp[cnt['t'] % 8]

        def nextS():
            cnt['s'] += 1
            return small[cnt['s'] % 4]

        def nextPl():
            cnt['pl'] += 1
            return pool4[cnt['pl'] % 4]

        def evac_eng():
            cnt['e'] += 1
            if EVAC_FORCE:
                return EVAC_FORCE
            return 'act' if cnt['e'] % 2 else 'dve'

        def copy_op(eng, out, in_, reads, writes):
            if eng == 'act':
                S.add('act', lambda e: e.activation(out, in_, AF.Copy), reads, writes)
            else:
                S.add(eng, lambda e: e.tensor_copy(out, in_), reads, writes)

        def cload(tv, c0, ncol, src, npart=128):
            S.add('sp', lambda e: e.dma_start(out=tv.ap(c0, [[1, ncol]], npart=npart), in_=src),
                  writes=[tv.r(c0, c0 + ncol)], dma='const')

        for i in range(4):
            cload(M1, 2048 * i, 2048, M1_d[:, 2048 * i:2048 * (i + 1)])
        for i in range(2):
            cload(IM, 2048 * i, 2048, IM_d[:, 2048 * i:2048 * (i + 1)])
        cload(S2m, 0, 384, S2_d[:, :])
        cload(R12, 0, 512, R12_d[:, :])
        cload(ident, 0, 128, id_d[:, :])
        cload(pv, 0, NPV, pv_d[:, :])
        cload(negt, 0, 32, negt_d[:, :])
        cload(mlpw, 0, 64, w1_d[:, :], npart=33)
        cload(mlpw, 64, 64, w2_d[:, :], npart=64)
        cload(mlpw, 128, 64, w3_d[:, :], npart=64)
        cload(mlpp, 0, 6, mp_d[:, :], npart=64)
        zT = Abuf
        for i in range(2):
            cload(zT, 2048 * i, 2048, zT_d[:, 2048 * i:2048 * (i + 1)], npart=33)
        for i in range(2):
            S.add('sp', lambda e, i=i: e.dma_start(out=xt[i].ap(0, [[1, 1024]], npart=64),
                                                   in_=wf_d[:, 1024 * i:1024 * (i + 1)]),
                  writes=[xt[i].r()], dma='xt%d' % i)
            S.add('dve', lambda e, i=i: e.tensor_copy(Wf.ap(1024 * i, [[1, 1024]], npart=64),
                                                      xt[i].ap(0, [[1, 1024]], npart=64)),
                  reads=[xt[i].r()], writes=[Wf.r(1024 * i, 1024 * (i + 1))])
        S.add('dve', lambda e: e.memset(onesm.ap(0, [[1, 128]]), 1.0 / 128.0), writes=[onesm.r()])
        epsb = sb("epsb", 2, F32)
        S.add('dve', lambda e: e.memset(epsb.ap(0, [[1, 2]]), EPS), writes=[epsb.r()])
        S.add('dve', lambda e: e.tensor_tensor(frb.ap(0, [[1, 3]], npart=64), mlpp.ap(0, [[1, 3]], npart=64),
                                               mlpp.ap(3, [[1, 3]], npart=64), ALU.mult),
              reads=[mlpp.r()], writes=[frb.r()])
        S.add('dve', lambda e: e.tensor_scalar(frb.ap(0, [[1, 3]], npart=64), frb.ap(0, [[1, 3]], npart=64),
                                               1.0 / TWO_PI, None, ALU.mult), reads=[frb.r()], writes=[frb.r()])
        S.add('dve', lambda e: e.tensor_scalar(mlpp.ap(3, [[1, 3]], npart=64), mlpp.ap(3, [[1, 3]], npart=64),
                                               1.0 / TWO_PI, None, ALU.mult), reads=[mlpp.r(), frb.r()], writes=[mlpp.r()])

        hA = Kbuf.cast(F32, 4)
        hB = XTbuf.cast(F32, 4)

        def mlp_layer(src_tv, src_np, wcol, li, dst_tv, dst_bf):
            for ti in range(8):
                c0 = 512 * ti
                pt = nextP()
                S.add('pe', lambda e, pt=pt, c0=c0: e.matmul(
                    pt.ap(0, [[1, 512]], npart=64), mlpw.ap(wcol, [[1, 64]], npart=src_np),
                    src_tv.ap(c0, [[1, 512]], npart=src_np), start=True, stop=True),
                    reads=[mlpw.r(wcol, wcol + 64), src_tv.r(c0, c0 + 512)], writes=[pt.r()])
                u = nextT()
                k = nextT()
                S.add('act', lambda e, pt=pt, u=u: e.activation(
                    u.ap(0, [[1, 512]], npart=64), pt.ap(0, [[1, 512]], npart=64), AF.Identity,
                    bias=frb.ap(li, [[1, 1]], npart=64), scale=mlpp.ap(3 + li, [[1, 1]], npart=64)),
                    reads=[pt.r(), mlpp.r(), frb.r()], writes=[u.r()])
                S.add('dve', lambda e, u=u, k=k: e.tensor_scalar(
                    k.ap(0, [[1, 512]], npart=64), u.ap(0, [[1, 512]], npart=64),
                    MAGIC, -MAGIC, ALU.add, ALU.add), reads=[u.r()], writes=[k.r()])
                S.add('dve', lambda e, u=u, k=k: e.tensor_tensor(
                    u.ap(0, [[1, 512]], npart=64), u.ap(0, [[1, 512]], npart=64),
                    k.ap(0, [[1, 512]], npart=64), ALU.subtract), reads=[u.r(), k.r()], writes=[u.r()])
                S.add('act', lambda e, u=u, c0=c0: e.activation(
                    dst_tv.ap(c0, [[1, 512]], npart=64), u.ap(0, [[1, 512]], npart=64), AF.Sin, scale=6.283185),
                    reads=[u.r()], writes=[dst_tv.r(c0, c0 + 512)])

        if 'F' in PH:
            mlp_layer(zT, 33, 0, 0, hA, False)
            mlp_layer(hA, 64, 64, 1, hB, False)
            mlp_layer(hB, 64, 128, 2, HmT, True)
        if dbg:
            S.add('sp', lambda e: e.dma_start(out=dbg_hm[:, :], in_=HmT.ap(0, [[1, L]], npart=64)),
                  reads=[HmT.r()], writes=[("dbg1", 0, 1)], dma='dbg')

        xt8 = [Kbuf.cast(F32, 4).sub(1024 * i, 1024) for i in range(4)] + [Abuf.sub(1024 * i, 1024) for i in range(4)]
        xnb8 = [XTbuf.sub(1024 * i, 1024) for i in range(8)]
        for g in (range(8) if 'X' in PH else []):
            sm = small[g % 4]
            for i in range(4):
                tt = 4 * g + i
                xs = xt8[(4 * g + i) % 8]
                S.add('sp', lambda e, xs=xs, tt=tt: e.dma_start(out=xs.ap(0, [[1, 1024]]),
                                                                in_=x_d[128 * tt:128 * (tt + 1), :]),
                      writes=[xs.r()], dma='xt8_%d' % ((4 * g + i) % 8))
            for i in range(4):
                xs = xt8[(4 * g + i) % 8]
                jt = nextT()
                S.add('act', lambda e, xs=xs, jt=jt, sm=sm, i=i: e.activation(
                    jt.cast(BF16, 2).ap(0, [[1, 1024]]), xs.ap(0, [[1, 1024]]), AF.Square,
                    accum_out=sm.ap(i, [[1, 1]])), reads=[xs.r()], writes=[jt.r(), sm.r(i, i + 1)])
            S.add('act', lambda e, sm=sm: e.activation(sm.ap(4, [[1, 4]]), sm.ap(0, [[1, 4]]), AF.Ln,
                                                       bias=epsb.ap(0, [[1, 1]]), scale=1.0 / D),
                  reads=[sm.r(0, 4), epsb.r()], writes=[sm.r(4, 8)])
            S.add('act', lambda e, sm=sm: e.activation(sm.ap(4, [[1, 4]]), sm.ap(4, [[1, 4]]), AF.Exp, scale=-0.5),
                  reads=[sm.r(4, 8)], writes=[sm.r(4, 8)])
            for i in range(4):
                tt = 4 * g + i
                xs = xt8[(4 * g + i) % 8]
                xb = xnb8[(4 * g + i) % 8]
                S.add('dve', lambda e, xs=xs, xb=xb, sm=sm, i=i: e.tensor_scalar(
                    xb.ap(0, [[1, 1024]]), xs.ap(0, [[1, 1024]]), sm.ap(4 + i, [[1, 1]]), None, ALU.mult),
                    reads=[xs.r(), sm.r(4 + i, 5 + i)], writes=[xb.r()])
                q = nextQ()
                for dc in range(8):
                    S.add('pe', lambda e, q=q, xb=xb, dc=dc: e.transpose(
                        q.ap(128 * dc, [[1, 128]]), xb.ap(128 * dc, [[1, 128]]), ident.ap(0, [[1, 128]])),
                        reads=[xb.r(128 * dc, 128 * dc + 128), ident.r()], writes=[q.r(128 * dc, 128 * dc + 128)])
                copy_op('dve', xnT.ap(128 * tt, [[L, 8], [1, 128]]), q.ap(0, [[128, 8], [1, 128]]),
                        [q.r()], [xnT.r(dc * L + 128 * tt, dc * L + 128 * tt + 128) for dc in range(8)])

        if dbg:
            for i in range(8):
                S.add('sp', lambda e, i=i: e.dma_start(out=dbg_xn[:, L * i:L * (i + 1)], in_=xnT.ap(L * i, [[1, L]])),
                      reads=[xnT.r(L * i, L * (i + 1))], writes=[("dbg2", i, i + 1)], dma='dbg')
        win_v = win_d.rearrange("(dc dl) e -> dl dc e", dl=128)

        wseq = []
        if 'C' in PH:
            wseq += [1024, 0]
            for cc_ in range(ncc):
                wseq += [2048 + 128 * cc_]
                if cc_ + 1 < ncc:
                    wseq += [1024 + 128 * (cc_ + 1), 128 * (cc_ + 1)]
        if 'H' in PH:
            for cc_ in range(nch):
                wseq += [4096 + 128 * cc_, 5120 + 128 * cc_, 3072 + 128 * cc_, 6144 + 128 * cc_]
        wstate = {'issued': 0, 'ptr': 0}

        def w_issue_upto(k):
            while wstate['issued'] < min(k, len(wseq)):
                i = wstate['issued']
                e0 = wseq[i]
                ws = wst[i % 2]
                S.add('sp', lambda e, ws=ws, e0=e0: e.dma_start(out=ws.ap(0, [[128, 8], [1, 128]]),
                                                              in_=win_v[:, :, e0:e0 + 128]),
                      writes=[ws.r()], dma=ws.base)
                wstate['issued'] += 1

        def load_w(e0):
            i = wstate['ptr']
            assert wseq[i] == e0, (i, wseq[i], e0)
            w_issue_upto(i + 2)
            ws, wb = wst[i % 2], wbf[i % 2]
            S.add('pool', lambda e: e.tensor_tensor(
                wb.ap(0, [[128, 8], [1, 128]]), ws.ap(0, [[128, 8], [1, 128]]),
                pv.ap(0, [[1, 8], [0, 128]]), ALU.mult), reads=[ws.r(), pv.r(0, 8)], writes=[wb.r()])
            wstate['ptr'] += 1
            return wb

        def inproj(wb, t0, n, pt):
            for dc in range(8):
                S.add('pe', lambda e, dc=dc: e.matmul(
                    pt.ap(0, [[1, n]]), wb.ap(128 * dc, [[1, 128]]), xnT.ap(dc * L + t0, [[1, n]]),
                    start=(dc == 0), stop=(dc == 7)),
                    reads=[wb.r(128 * dc, 128 * dc + 128), xnT.r(dc * L + t0, dc * L + t0 + n)],
                    writes=[pt.r(0, n)])

        def groups(lo, hi):
            out = []
            while lo < hi:
                n = min(512, hi - lo)
                out.append((lo, n))
                lo += n
            return out

        def groups_bal(lo, hi):
            n = hi - lo
            k = (n + 511) // 512
            base = n // k
            out = []
            for i in range(k):
                sz = base + (1 if i < n - base * k else 0)
                out.append((lo, sz))
                lo += sz
            return out

        def ln_tile(src_ap, src_reads, dst, n=512):
            S.add('act', lambda e: e.activation(dst.ap(0, [[1, n]]), src_ap, AF.Ln, bias=epsb.ap(0, [[1, 1]])),
                  reads=src_reads + [epsb.r()], writes=[dst.r(0, n)])

        def exph_tile(dst, n=512):
            S.add('act', lambda e: e.activation(dst.ap(0, [[1, n]]), dst.ap(0, [[1, n]]), AF.Exp, scale=-0.5),
                  reads=[dst.r(0, n)], writes=[dst.r(0, n)])


        sgb = accbuf
        dg = XTbuf
        S.add('dve', lambda e: e.memset(a_pad.ap(0, [[1, 16]]), 0.0), writes=[a_pad.r(0, 16)])
        Kf32 = Kbuf.cast(F32, 4)
        c_ac, c_dd = Abuf.sub(0, 2048), Abuf.sub(2048, 2048)
        c_sq, c_sz = Kf32.sub(0, 2048), Kf32.sub(2048, 2048)
        conf_chunks = list(range(ncc)) if 'C' in PH else []

        def conf_prep_dg(cc):
            S.add('dve', lambda e, cc=cc: e.tensor_tensor(
                dg.ap(0, [[128, 31], [1, 128]]), ident.ap(0, [[0, 31], [1, 128]]),
                pv.ap(8 + cc * 31, [[1, 31], [0, 128]]), ALU.mult),
                reads=[ident.r(), pv.r(8 + cc * 31, 8 + cc * 31 + 31)], writes=[dg.r(0, 31 * 128)])

        def conf_prep_gate(cc):
            wg = load_w(1024 + 128 * cc)
            for (t0, n) in groups_bal(0, 2063):
                pt = nextP()
                inproj(wg, t0, n, pt)
                S.add('act', lambda e, pt=pt, t0=t0, n=n: e.activation(
                    sgb.ap(t0, [[1, n]]), pt.ap(0, [[1, n]]), AF.Sigmoid),
                    reads=[pt.r(0, n)], writes=[sgb.r(t0, t0 + n)])

        def conf_prep_val(cc):
            wv = load_w(128 * cc)
            for (t0, n) in groups_bal(0, 2063):
                pt = nextP()
                inproj(wv, t0, n, pt)
                S.add('dve', lambda e, pt=pt, t0=t0, n=n: e.tensor_tensor(
                    a_pad.ap(15 + t0, [[1, n]]), pt.ap(0, [[1, n]]), sgb.ap(t0, [[1, n]]), ALU.mult),
                    reads=[pt.r(0, n), sgb.r(t0, t0 + n)], writes=[a_pad.r(15 + t0, 15 + t0 + n)])

        if conf_chunks:
            conf_prep_dg(0)
            conf_prep_gate(0)
            conf_prep_val(0)
        for ci, cc in enumerate(conf_chunks):
            nxt = conf_chunks[ci + 1] if ci + 1 < len(conf_chunks) else None
            wz = load_w(2048 + 128 * cc)
            for ti in range(4):
                t0 = 512 * ti
                pc = nextP()
                for k in range(31):
                    S.add('pe', lambda e, pc=pc, k=k, t0=t0: e.matmul(
                        pc.ap(0, [[1, 512]]), dg.ap(128 * k, [[1, 128]]), a_pad.ap(t0 + k, [[1, 512]]),
                        start=(k == 0), stop=(k == 30)),
                        reads=[dg.r(128 * k, 128 * k + 128), a_pad.r(t0 + k, t0 + k + 512)], writes=[pc.r()])
                S.add('act', lambda e, pc=pc, t0=t0, cc=cc: e.activation(
                    c_ac.ap(t0, [[1, 512]]), pc.ap(0, [[1, 512]]), AF.Identity, bias=pv.ap(256 + cc, [[1, 1]])),
                    reads=[pc.r(), pv.r(256 + cc, 257 + cc)], writes=[c_ac.r(t0, t0 + 512)])
            for ti in range(4):
                t0 = 512 * ti
                pm = nextP()
                S.add('pe', lambda e, pm=pm, t0=t0: e.matmul(pm.ap(0, [[1, 512]]), onesm.ap(0, [[1, 128]]),
                                                            c_ac.ap(t0, [[1, 512]]), start=True, stop=True),
                      reads=[onesm.r(), c_ac.r(t0, t0 + 512)], writes=[pm.r()])
                S.add('dve', lambda e, pm=pm, t0=t0: e.tensor_tensor(
                    c_dd.ap(t0, [[1, 512]]), c_ac.ap(t0, [[1, 512]]), pm.ap(0, [[1, 512]]), ALU.subtract),
                    reads=[c_ac.r(t0, t0 + 512), pm.r()], writes=[c_dd.r(t0, t0 + 512)])
                S.add('act', lambda e, t0=t0: e.activation(c_sq.ap(t0, [[1, 512]]), c_dd.ap(t0, [[1, 512]]), AF.Square),
                      reads=[c_dd.r(t0, t0 + 512)], writes=[c_sq.r(t0, t0 + 512)])
            if nxt is not None:
                conf_prep_dg(nxt)
                conf_prep_gate(nxt)
            for ti in range(4):
                t0 = 512 * ti
                pz = nextP()
                inproj(wz, t0, 512, pz)
                S.add('act', lambda e, pz=pz, t0=t0: e.activation(c_sz.ap(t0, [[1, 512]]), pz.ap(0, [[1, 512]]), AF.Silu),
                      reads=[pz.r()], writes=[c_sz.r(t0, t0 + 512)])
            for ti in range(4):
                t0 = 512 * ti
                pvv = nextP()
                S.add('pe', lambda e, pvv=pvv, t0=t0: e.matmul(pvv.ap(0, [[1, 512]]), onesm.ap(0, [[1, 128]]),
                                                              c_sq.ap(t0, [[1, 512]]), start=True, stop=True),
                      reads=[onesm.r(), c_sq.r(t0, t0 + 512)], writes=[pvv.r()])
                ln_tile(pvv.ap(0, [[1, 512]]), [pvv.r()], c_sq.sub(t0, 512))
            for ti in range(4):
                t0 = 512 * ti
                exph_tile(c_sq.sub(t0, 512))
                S.add('dve', lambda e, t0=t0: e.tensor_tensor(
                    c_dd.ap(t0, [[1, 512]]), c_dd.ap(t0, [[1, 512]]), c_sq.ap(t0, [[1, 512]]), ALU.mult),
                    reads=[c_dd.r(t0, t0 + 512), c_sq.r(t0, t0 + 512)], writes=[c_dd.r(t0, t0 + 512)])
            for ti in range(4):
                t0 = 512 * ti
                S.add('act', lambda e, t0=t0, cc=cc: e.activation(
                    c_dd.ap(t0, [[1, 512]]), c_dd.ap(t0, [[1, 512]]), AF.Silu,
                    bias=pv.ap(272 + cc, [[1, 1]]), scale=pv.ap(264 + cc, [[1, 1]])),
                    reads=[c_dd.r(t0, t0 + 512), pv.r(264 + cc, 273 + cc)], writes=[c_dd.r(t0, t0 + 512)])
            if nxt is not None:
                conf_prep_val(nxt)
            for ti in range(4):
                t0 = 512 * ti
                mo = nextPl()
                S.add('dve', lambda e, t0=t0, mo=mo: e.tensor_tensor(
                    mo.ap(0, [[1, 512]]), c_dd.ap(t0, [[1, 512]]), c_sz.ap(t0, [[1, 512]]), ALU.mult),
                    reads=[c_dd.r(t0, t0 + 512), c_sz.r(t0, t0 + 512)], writes=[mo.r(0, 512)])
                S.add('sp', lambda e, mo=mo, cc=cc, ti=ti: e.dma_start(
                    out=mix_w[:, 4 * ti:4 * ti + 4, cc, :], in_=mo.ap(0, [[128, 4], [1, 128]])),
                    reads=[mo.r(0, 512)], writes=[("mixd", 0, 1)], dma='mw_' + mo.base)

        XTs = XTbuf
        XTd_off = 4096
        Kre_off, Kim_off = 0, 4096
        hx0 = accbuf
        accv_off, accx_off = 0, 1024
        yb = accbuf

        raw4 = [raw[0], raw[1], accbuf.sub(0, 1040), accbuf.sub(1040, 1040)]

        rawb = [raw[0].cast(BF16, 2).sub(0, 1040), raw[0].cast(BF16, 2).sub(1040, 1040),
                raw[1].cast(BF16, 2).sub(0, 1040), raw[1].cast(BF16, 2).sub(1040, 1040)]
        dg3 = [mlpw.cast(BF16, 2), sb("dg3b", 384, BF16)]

        def build_dg3(slot, j):
            wc = 280 + 3 * j
            S.add('dve', lambda e: e.tensor_tensor(
                dg3[slot].ap(0, [[128, 3], [1, 128]]), ident.ap(0, [[0, 3], [1, 128]]),
                pv.ap(wc, [[1, 3], [0, 128]]), ALU.mult),
                reads=[ident.r(), pv.r(wc, wc + 3)], writes=[dg3[slot].r(0, 384)])

        def shortconv_quarter(wb, j, q, dst_tv, dst_off, dslot):
            cnt['x'] += 1
            rw = rawb[cnt['x'] % 4]
            jlo, jhi = 0, 1026
            if q == 0:
                jlo = 1
                S.add('dve', lambda e: e.memset(rw.ap(0, [[1, 1]]), 0.0), writes=[rw.r(0, 1)])
            if q == 3:
                jhi = 1025
                S.add('dve', lambda e: e.memset(rw.ap(1025, [[1, 1]]), 0.0), writes=[rw.r(1025, 1026)])
            bc = 352 + j
            for (j0, n) in groups_bal(jlo, jhi):
                t0 = 1024 * q - 1 + j0
                pt = nextP()
                inproj(wb, t0, n, pt)
                S.add('act', lambda e, pt=pt, j0=j0, n=n: e.activation(rw.ap(j0, [[1, n]]), pt.ap(0, [[1, n]]), AF.Copy),
                      reads=[pt.r(0, n)], writes=[rw.r(j0, j0 + n)])
                yield ('sc', j0)
            dg = dg3[dslot]
            for h in range(2):
                pc = nextP()
                for k in range(3):
                    S.add('pe', lambda e, pc=pc, k=k, h=h: e.matmul(
                        pc.ap(0, [[1, 512]]), dg.ap(128 * k, [[1, 128]]), rw.ap(512 * h + k, [[1, 512]]),
                        start=(k == 0), stop=(k == 2)),
                        reads=[dg.r(128 * k, 128 * k + 128), rw.r(512 * h + k, 512 * h + k + 512)], writes=[pc.r()])
                S.add('act', lambda e, pc=pc, h=h: e.activation(
                    dst_tv.ap(dst_off + 512 * h, [[1, 512]]), pc.ap(0, [[1, 512]]), AF.Identity,
                    bias=pv.ap(bc, [[1, 1]])),
                    reads=[pc.r(), pv.r(bc, bc + 1)],
                    writes=[dst_tv.r(dst_off + 512 * h, dst_off + 512 * h + 512)])
            yield ('tap', q)

        def transform(xt_off, consumer, need='both', bg_eng=None):
            for bp in range(16):
                pt = nextP()
                for bb in range(2):
                    b = 2 * bp + bb
                    S.add('pe', lambda e, pt=pt, b=b, bb=bb: e.matmul(
                        pt.ap(256 * bb, [[1, 256]]), XTbuf.ap(xt_off + 128 * b, [[1, 128]]),
                        M1.ap(256 * b, [[1, 256]]), start=True, stop=True),
                        reads=[XTbuf.r(xt_off + 128 * b, xt_off + 128 * b + 128), M1.r(256 * b, 256 * b + 256)],
                        writes=[pt.r(256 * bb, 256 * bb + 256)])
                for bb in range(2):
                    b = 2 * bp + bb
                    copy_op('dve', Abf.ap(4 * b, [[4096, 2], [128, 32], [1, 4]]),
                            pt.ap(256 * bb, [[128, 2], [4, 32], [1, 4]]),
                            [pt.r(256 * bb, 256 * bb + 256)], [Abf.r(0, 8192)])
                yield ('s1', bp)
            def stage_T(g):
                q = nextQ()
                for ri in range(2):
                    for j in range(4):
                        k1hi = 4 * g + j
                        col = (ri * 4 + j) * 128
                        S.add('pe', lambda e, q=q, ri=ri, k1hi=k1hi, col=col: e.transpose(
                            q.ap(col, [[1, 128]]), Abf.ap(ri * 4096 + 128 * k1hi, [[1, 128]]),
                            ident.ap(0, [[1, 128]])),
                            reads=[Abf.r(0, 8192), ident.r()], writes=[q.r(col, col + 128)])
                bg = nextPl()
                copy_op(bg_eng or evac_eng(), bg.ap(0, [[1, 1024]]), q.ap(0, [[1, 1024]]), [q.r()], [bg.r()])
                return bg

            def stage_S(g, bg):
                pre = nextP() if need in ('both', 're') else None
                pim = nextP() if need in ('both', 'im') else None
                mm = []
                if pre is not None:
                    mm += [(pre, 0, 0, True), (pre, 128, 512, False)]
                if pim is not None:
                    mm += [(pim, 0, 512, True), (pim, 256, 0, False)]
                for (po, mcol, bcol, stt) in mm:
                    S.add('pe', lambda e, po=po, mcol=mcol, bcol=bcol, stt=stt, bg=bg: e.matmul(
                        po.ap(0, [[1, 512]]), S2m.ap(mcol, [[1, 128]]), bg.ap(bcol, [[1, 512]]),
                        start=stt, stop=(not stt)),
                        reads=[S2m.r(), bg.r(bcol, bcol + 512)], writes=[po.r()])
                return consumer(g, pre, pim)

            bgs = {0: stage_T(0)}
            pend = None
            for g in range(8):
                if g + 1 < 8:
                    bgs[g + 1] = stage_T(g + 1)
                d = stage_S(g, bgs.pop(g))
                if pend is not None:
                    pend()
                pend = d
                yield ('pc', g)
            if pend is not None:
                pend()

        hy_chunks = list(range(nch)) if 'H' in PH else []

        def load_absd(cc):
            for d_ in range(2):
                S.add('sp', lambda e, d_=d_, cc=cc: e.dma_start(
                    out=absd.ap(128 * d_, [[128, 1], [1, 128]]),
                    in_=del_d[d_:d_ + 1, 128 * cc:128 * cc + 128].partition_broadcast(128)),
                    writes=[absd.r(128 * d_, 128 * d_ + 128)], dma='absd')
            S.add('act', lambda e: e.activation(absd.ap(0, [[1, 256]]), absd.ap(0, [[1, 256]]), AF.Abs),
                  reads=[absd.r()], writes=[absd.r()])

        def load_skipb(cc):
            S.add('sp', lambda e, cc=cc: e.dma_start(
                out=skipb.ap(0, [[128, 1], [1, 128]]), in_=skip_d[0:1, 128 * cc:128 * cc + 128].partition_broadcast(128)),
                writes=[skipb.r()], dma='skipb')

        def filter_evac_gen(cc):
            for gb in range(8):
                ph = [nextP(), nextP()]
                for d_ in range(2):
                    for j in range(4):
                        b = 4 * gb + j
                        S.add('pe', lambda e, d_=d_, j=j, b=b, cc=cc, ph=ph: e.matmul(
                            ph[d_].ap(128 * j, [[1, 128]]), HmT.ap(b, [[32, 128]], npart=64),
                            Wf.ap(1024 * d_ + 128 * cc, [[1, 128]], npart=64), start=True, stop=True),
                            reads=[HmT.r(), Wf.r(1024 * d_ + 128 * cc, 1024 * d_ + 128 * cc + 128)],
                            writes=[ph[d_].r(128 * j, 128 * j + 128)])
                cnt['dp'] = cnt.get('dp', 0) + 1
                dec2 = tmpall.sub(1024 * (cnt['dp'] % 4), 1024)
                for j in range(4):
                    b = 4 * gb + j
                    S.add('act', lambda e, dec2=dec2, j=j, b=b: e.activation(
                        dec2.ap(256 * j, [[1, 256]]), absd.ap(0, [[1, 256]]), AF.Exp,
                        scale=negt.ap(b, [[1, 1]])),
                        reads=[negt.r(), absd.r(0, 256)], writes=[dec2.r(256 * j, 256 * j + 256)])
                for d_ in range(2):
                    S.add('dve', lambda e, dec2=dec2, d_=d_, ph=ph: e.tensor_tensor(
                        dec2.ap(128 * d_, [[256, 4], [1, 128]]), ph[d_].ap(0, [[128, 4], [1, 128]]),
                        dec2.ap(128 * d_, [[256, 4], [1, 128]]), ALU.mult),
                        reads=[ph[d_].r(), dec2.r()], writes=[dec2.r()])
                S.add('dve', lambda e, dec2=dec2, gb=gb: e.tensor_tensor(
                    XTbuf.ap(512 * gb, [[128, 4], [1, 128]]), dec2.ap(0, [[256, 4], [1, 128]]),
                    dec2.ap(128, [[256, 4], [1, 128]]), ALU.add),
                    reads=[dec2.r()], writes=[XTbuf.r(512 * gb, 512 * gb + 512)])
                S.add('dve', lambda e, dec2=dec2, gb=gb: e.tensor_tensor(
                    XTbuf.ap(XTd_off + 512 * gb, [[128, 4], [1, 128]]), dec2.ap(0, [[256, 4], [1, 128]]),
                    dec2.ap(128, [[256, 4], [1, 128]]), ALU.subtract),
                    reads=[dec2.r()], writes=[XTbuf.r(XTd_off + 512 * gb, XTd_off + 512 * gb + 512)])
                if gb == 0:
                    S.add('dve', lambda e: e.tensor_tensor(
                        XTbuf.ap(0, [[1, 128]], npart=1), XTbuf.ap(0, [[1, 128]], npart=1),
                        skipb.ap(0, [[1, 128]], npart=1), ALU.add),
                        reads=[XTbuf.r(0, 128), skipb.r()], writes=[XTbuf.r(0, 128)])
                yield gb
            if cc + 1 < len(hy_chunks):
                load_absd(cc + 1)
                load_skipb(cc + 1)

        if hy_chunks:
            load_absd(0)
            load_skipb(0)
            for _ in filter_evac_gen(0):
                pass
        for cc in hy_chunks:
            if hstop <= 2:
                continue
            def cons_s(g, pre, pim):
                copy_op('act' if g % 2 else 'dve', Kbuf.ap(Kre_off + 512 * g, [[1, 512]]), pre.ap(0, [[1, 512]]),
                        [pre.r()], [Kbuf.r(Kre_off + 512 * g, Kre_off + 512 * g + 512)])

            def cons_d(g, pre, pim):
                copy_op('dve' if g % 2 else 'act', Kbuf.ap(Kim_off + 512 * g, [[1, 512]]), pim.ap(0, [[1, 512]]),
                        [pim.r()], [Kbuf.r(Kim_off + 512 * g, Kim_off + 512 * g + 512)])

            raw2 = [raw[0], raw[1]]

            def vx1_gen():
                for q4 in range(4):
                    yield from shortconv_quarter(wx1, 8 + cc, q4, accbuf, 1040, 0)
                    yield from shortconv_quarter(wvv, 16 + cc, q4, accbuf, 0, 1)
                    S.add('dve', lambda e, q4=q4: e.tensor_tensor(
                        XTbuf.ap(1024 * q4, [[1, 1024]]), accbuf.ap(0, [[1, 1024]]),
                        accbuf.ap(1040, [[1, 1024]]), ALU.mult),
                        reads=[accbuf.r(0, 1024), accbuf.r(1040, 2064)],
                        writes=[XTbuf.r(1024 * q4, 1024 * q4 + 1024)])
                    yield ('w', q4)

            gs = transform(0, cons_s, 're')
            for _ in range(10):
                next(gs)
            wx1 = load_w(4096 + 128 * cc)
            wvv = load_w(5120 + 128 * cc)
            build_dg3(0, 8 + cc)
            build_dg3(1, 16 + cc)
            gv = vx1_gen()

            def chain2():
                yield from gs
                yield from transform(XTd_off, cons_d, 'im')

            gf = chain2()
            while True:
                a_ = next(gf, None)
                b_ = next(gv, None)
                if a_ is None and b_ is None:
                    break

            if hstop <= 3:
                continue
            if hstop <= 4:
                continue
            for gq in range(4):
                q = nextQ()
                for j in range(8):
                    b = 8 * gq + j
                    S.add('pe', lambda e, q=q, j=j, b=b: e.transpose(
                        q.ap(128 * j, [[1, 128]]), XTbuf.ap(b, [[32, 128]]), ident.ap(0, [[1, 128]])),
                        reads=[XTbuf.r(0, 4096), ident.r()], writes=[q.r(128 * j, 128 * j + 128)])
                copy_op('dve', XTbuf.ap(XTd_off + 1024 * gq, [[1, 1024]]), q.ap(0, [[1, 1024]]),
                        [q.r()], [XTbuf.r(XTd_off + 1024 * gq, XTd_off + 1024 * gq + 1024)])

            if hstop <= 5:
                continue

            def cons_x(g, pre, pim):
                kre = Kbuf.ap(Kre_off + 512 * g, [[1, 512]])
                kim = Kbuf.ap(Kim_off + 512 * g, [[1, 512]])
                kr_r = Kbuf.r(Kre_off + 512 * g, Kre_off + 512 * g + 512)
                ki_r = Kbuf.r(Kim_off + 512 * g, Kim_off + 512 * g + 512)
                yg = nextPl()
                xr = pre
                xi = pim
                t1, t2 = nextT(), nextT()
                t3, t4 = nextT(), nextT()
                S.add('dve', lambda e: e.tensor_tensor(t1.ap(0, [[1, 512]]), xr.ap(0, [[1, 512]]), kre, ALU.mult),
                      reads=[xr.r(), kr_r], writes=[t1.r()])
                S.add('dve', lambda e: e.tensor_tensor(t3.ap(0, [[1, 512]]), xr.ap(0, [[1, 512]]), kim, ALU.mult),
                      reads=[xr.r(), ki_r], writes=[t3.r()])
                S.add('dve', lambda e: e.tensor_tensor(t2.ap(0, [[1, 512]]), xi.ap(0, [[1, 512]]), kim, ALU.mult),
                      reads=[xi.r(), ki_r], writes=[t2.r()])
                S.add('dve', lambda e: e.tensor_tensor(t4.ap(0, [[1, 512]]), xi.ap(0, [[1, 512]]), kre, ALU.mult),
                      reads=[xi.r(), kr_r], writes=[t4.r()])
                S.add('dve', lambda e: e.tensor_tensor(yg.ap(0, [[1, 512]]), t1.ap(0, [[1, 512]]),
                                                       t2.ap(0, [[1, 512]]), ALU.subtract),
                      reads=[t1.r(), t2.r()], writes=[yg.r(0, 512)])
                S.add('pool', lambda e: e.tensor_tensor(yg.ap(512, [[1, 512]]), t3.ap(0, [[1, 512]]),
                                                        t4.ap(0, [[1, 512]]), ALU.add),
                      reads=[t3.r(), t4.r()], writes=[yg.r(512, 1024)])
                def deferred():
                    for jp in range(2):
                        pt = nextP()
                        for jj in range(2):
                            j = 2 * jp + jj
                            S.add('pe', lambda e, pt=pt, jj=jj, j=j: e.matmul(
                                pt.ap(256 * jj, [[1, 256]]), yg.ap(128 * j, [[1, 128]]), R12.ap(0, [[1, 256]]),
                                start=True, stop=False),
                                reads=[yg.r(128 * j, 128 * j + 128), R12.r()],
                                writes=[pt.r(256 * jj, 256 * jj + 256)])
                            S.add('pe', lambda e, pt=pt, jj=jj, j=j: e.matmul(
                                pt.ap(256 * jj, [[1, 256]]), yg.ap(512 + 128 * j, [[1, 128]]), R12.ap(256, [[1, 256]]),
                                start=False, stop=True),
                                reads=[yg.r(512 + 128 * j, 512 + 128 * j + 128), R12.r()],
                                writes=[pt.r(256 * jj, 256 * jj + 256)])
                        for jj in range(2):
                            k1hi = 4 * g + 2 * jp + jj
                            copy_op('act', XTbuf.ap(4 * k1hi, [[128, 2], [256, 32], [1, 4]]),
                                    pt.ap(256 * jj, [[128, 2], [4, 32], [1, 4]]),
                                    [pt.r(256 * jj, 256 * jj + 256)], [XTbuf.r(0, 8192)])
                return deferred

            wx0 = load_w(3072 + 128 * cc)

            def x0_gen():
                for q4 in range(2):
                    yield from shortconv_quarter(wx0, cc, q4, hx0, 1024 * q4, 0)

            build_dg3(0, cc)
            gx = transform(XTd_off, cons_x, 'both', 'act')
            g0 = x0_gen()
            for _ in range(16):
                next(gx)
            while True:
                a_ = next(gx, None)
                b_ = next(g0, None)
                if a_ is None and b_ is None:
                    break
            Dsb = XTbuf

            if hstop <= 6:
                continue
            if hstop <= 7:
                continue
            PY = [P[0], P[1], P[2], P[3]]

            def it1_T(gb):
                q = nextQ()
                for j in range(4):
                    bp_ = 4 * gb + j
                    for ri in range(2):
                        col = (2 * j + ri) * 128
                        S.add('pe', lambda e, q=q, col=col, ri=ri, bp_=bp_: e.transpose(
                            q.ap(col, [[1, 128]]), Dsb.ap(bp_ * 256 + ri * 128, [[1, 128]]),
                            ident.ap(0, [[1, 128]])),
                            reads=[Dsb.r(0, 8192), ident.r()], writes=[q.r(col, col + 128)])
                dt_ = nextPl()
                copy_op('dve', dt_.ap(0, [[1, 1024]]), q.ap(0, [[1, 1024]]), [q.r()], [dt_.r()])
                return dt_

            def is1_S(gb, dt_):
                for j in range(4):
                    bp_ = 4 * gb + j
                    pyb = PY[bp_ // 8]
                    oc = (bp_ % 8) * 64
                    for ri in range(2):
                        S.add('pe', lambda e, pyb=pyb, oc=oc, j=j, ri=ri, bp_=bp_, dt_=dt_: e.matmul(
                            pyb.ap(oc, [[1, 64]]), dt_.ap((2 * j + ri) * 128, [[1, 128]]),
                            IM.ap((bp_ * 2 + ri) * 64, [[1, 64]]), start=(ri == 0), stop=(ri == 1)),
                            reads=[dt_.r((2 * j + ri) * 128, (2 * j + ri) * 128 + 128),
                                   IM.r((bp_ * 2 + ri) * 64, (bp_ * 2 + ri) * 64 + 64)],
                            writes=[pyb.r(oc, oc + 64)])

            dts = {0: it1_T(0)}
            for gb in range(8):
                if gb + 1 < 8:
                    dts[gb + 1] = it1_T(gb + 1)
                is1_S(gb, dts.pop(gb))
            if hstop <= 8:
                continue
            for kb in range(4):
                S.add('dve', lambda e, kb=kb: e.tensor_tensor(
                    yb.ap(8 * kb, [[32, 64], [1, 8]]), PY[kb].ap(0, [[1, 64], [64, 8]]),
                    hx0.ap(8 * kb, [[32, 64], [1, 8]]), ALU.mult),
                    reads=[PY[kb].r(), hx0.r(0, 2048)], writes=[yb.r(0, 2048)])
            if hstop <= 9:
                continue
            wz = load_w(6144 + 128 * cc)
            Kf = Kbuf.cast(F32, 4)
            fsq = [Kf.sub(512 * i, 512) for i in range(4)]
            fsz = [Abuf.sub(2048 + 512 * i, 512) for i in range(4)]
            fgen = filter_evac_gen(cc + 1) if cc + 1 < len(hy_chunks) else iter(())

            def adv(k):
                for _ in range(k):
                    next(fgen, None)

            for ti in range(4):
                t0 = 512 * ti
                S.add('act', lambda e, ti=ti, t0=t0: e.activation(fsq[ti].ap(0, [[1, 512]]), yb.ap(t0, [[1, 512]]), AF.Square),
                      reads=[yb.r(t0, t0 + 512)], writes=[fsq[ti].r()])
            adv(2)
            for ti in range(4):
                pvv = nextP()
                S.add('pe', lambda e, pvv=pvv, ti=ti: e.matmul(pvv.ap(0, [[1, 512]]), onesm.ap(0, [[1, 128]]),
                                                              fsq[ti].ap(0, [[1, 512]]), start=True, stop=True),
                      reads=[onesm.r(), fsq[ti].r()], writes=[pvv.r()])
                ln_tile(pvv.ap(0, [[1, 512]]), [pvv.r()], fsq[ti])
            adv(2)
            for ti in range(4):
                t0 = 512 * ti
                exph_tile(fsq[ti])
                S.add('dve', lambda e, ti=ti, t0=t0: e.tensor_tensor(
                    fsq[ti].ap(0, [[1, 512]]), fsq[ti].ap(0, [[1, 512]]), yb.ap(t0, [[1, 512]]), ALU.mult),
                    reads=[fsq[ti].r(), yb.r(t0, t0 + 512)], writes=[fsq[ti].r()])
            adv(2)
            for ti in range(4):
                t0 = 512 * ti
                pz = nextP()
                inproj(wz, t0, 512, pz)
                S.add('act', lambda e, pz=pz, ti=ti: e.activation(fsz[ti].ap(0, [[1, 512]]), pz.ap(0, [[1, 512]]), AF.Silu),
                      reads=[pz.r()], writes=[fsz[ti].r()])
            adv(1)
            for ti in range(4):
                t0 = 512 * ti
                mo = nextPl()
                S.add('dve', lambda e, ti=ti, mo=mo, cc=cc: e.scalar_tensor_tensor(
                    mo.ap(0, [[1, 512]]), fsq[ti].ap(0, [[1, 512]]), pv.ap(376 + cc, [[1, 1]]), fsz[ti].ap(0, [[1, 512]]),
                    ALU.mult, ALU.mult),
                    reads=[fsq[ti].r(), fsz[ti].r(), pv.r(376 + cc, 377 + cc)], writes=[mo.r(0, 512)])
                S.add('sp', lambda e, mo=mo, cc=cc, ti=ti: e.dma_start(
                    out=mix_w[:, 4 * ti:4 * ti + 4, 8 + cc, :], in_=mo.ap(0, [[128, 4], [1, 128]])),
                    reads=[mo.r(0, 512)], writes=[("mixd", 0, 1)], dma='mw_' + mo.base)
            adv(8)

        Wo = xnT
        wout_v = wout_d.rearrange("(mc ml) d -> ml mc d", ml=128)
        for mc in (range(16) if '3' in PH else []):
            ws = xt[mc % 2]
            S.add('sp', lambda e, ws=ws, mc=mc: e.dma_start(out=ws.ap(0, [[1, 1024]]), in_=wout_v[:, mc, :]),
                  writes=[ws.r()], dma='xt%d' % (mc % 2))
            copy_op(evac_eng(), Wo.ap(1024 * mc, [[1, 1024]]), ws.ap(0, [[1, 1024]]),
                    [ws.r()], [Wo.r(1024 * mc, 1024 * mc + 1024)])
        fgb = Kbuf.cast(F32, 4).sub(2048, 1024)
        S.add('sp', lambda e: e.dma_start(out=fgb.ap(0, [[1024, 1], [1, 1024]]), in_=fg_d[0:1, :].partition_broadcast(128)),
              writes=[fgb.r(0, 1024)], dma='fgb')

        hsb = Abuf
        xres = [accbuf.sub(0, 1024), accbuf.sub(1040, 1024), raw[0].sub(0, 1024), raw[1].sub(0, 1024)]
        for tt in (range(16) if '3' in PH else []):
            mt = XTbuf
            mo_ = (tt % 4) * 2048
            S.add('sp', lambda e, tt=tt, mo_=mo_: e.dma_start(
                out=mt.ap(mo_, [[1, 2048]]), in_=mix_d[128 * tt:128 * (tt + 1), :]),
                reads=[("mixd", 0, 1)], writes=[mt.r(mo_, mo_ + 2048)], dma='mixr%d' % (tt % 4))
            xr_ = xres[tt % 4]
            S.add('sp', lambda e, tt=tt, xr_=xr_: e.dma_start(out=xr_.ap(0, [[1, 1024]]),
                                                             in_=x_d[128 * tt:128 * (tt + 1), :]),
                  writes=[xr_.r()], dma='xres%d' % (tt % 4))
            ho = (tt % 2) * 2048
            for dh in range(2):
                pt = nextP()
                for mc in range(16):
                    S.add('pe', lambda e, pt=pt, mc=mc, dh=dh, mo_=mo_: e.matmul(
                        pt.ap(0, [[1, 512]]), mt.ap(mo_ + 128 * mc, [[1, 128]]),
                        Wo.ap(1024 * mc + 512 * dh, [[1, 512]]), start=(mc == 0), stop=(mc == 15)),
                        reads=[mt.r(mo_ + 128 * mc, mo_ + 128 * mc + 128),
                               Wo.r(1024 * mc + 512 * dh, 1024 * mc + 512 * dh + 512)], writes=[pt.r()])
                S.add('dve', lambda e, pt=pt, dh=dh, xr_=xr_, ho=ho: e.tensor_tensor(
                    hsb.ap(ho + 512 * dh, [[1, 512]]), pt.ap(0, [[1, 512]]), xr_.ap(512 * dh, [[1, 512]]),
                    ALU.add),
                    reads=[pt.r(), xr_.r(512 * dh, 512 * dh + 512)],
                    writes=[hsb.r(ho + 512 * dh, ho + 512 * dh + 512)])
            sm = nextS()
            S.add('act', lambda e, sm=sm, ho=ho: e.activation(
                hsb.ap(ho + 1024, [[1, 1024]]), hsb.ap(ho, [[1, 1024]]), AF.Square, accum_out=sm.ap(0, [[1, 1]])),
                reads=[hsb.r(ho, ho + 1024)], writes=[hsb.r(ho + 1024, ho + 2048), sm.r(0, 1)])
            S.add('dve', lambda e, sm=sm: e.tensor_scalar(sm.ap(1, [[1, 1]]), sm.ap(0, [[1, 1]]),
                                                          1.0 / D, EPS, ALU.mult, ALU.add),
                  reads=[sm.r(0, 1)], writes=[sm.r(1, 2)])
            S.add('act', lambda e, sm=sm: e.activation(sm.ap(2, [[1, 1]]), sm.ap(1, [[1, 1]]), AF.Sqrt),
                  reads=[sm.r(1, 2)], writes=[sm.r(2, 3)])
            S.add('dve', lambda e, sm=sm: e.reciprocal(sm.ap(3, [[1, 1]]), sm.ap(2, [[1, 1]])),
                  reads=[sm.r(2, 3)], writes=[sm.r(3, 4)])
            S.add('dve', lambda e, sm=sm, ho=ho: e.scalar_tensor_tensor(
                hsb.ap(ho + 1024, [[1, 1024]]), hsb.ap(ho, [[1, 1024]]), sm.ap(3, [[1, 1]]),
                fgb.ap(0, [[1, 1024]]), ALU.mult, ALU.mult),
                reads=[hsb.r(ho, ho + 1024), sm.r(3, 4), fgb.r(0, 1024)], writes=[hsb.r(ho + 1024, ho + 2048)])
            S.add('pool', lambda e, tt=tt, ho=ho: e.dma_start(out=out_d[128 * tt:128 * (tt + 1), :],
                                                              in_=hsb.ap(ho + 1024, [[1, 1024]])),
                  reads=[hsb.r(ho + 1024, ho + 2048)], writes=[("outd", tt, tt + 1)], dma='outw%d' % (tt % 2))

        S.emit(nc, (['outw0', 'outw1'] if '3' in PH else []) + (['dbg'] if dbg else []))
    return nc


def _bf(a):
    return np.ascontiguousarray(a.astype(np.float32)).astype(ml_dtypes.bfloat16)


def _consts():
    N = 8192
    th = 2 * np.pi / N
    a = np.arange(128)[:, None, None].astype(np.float64)
    b = np.arange(32)[None, :, None].astype(np.float64)
    k1 = np.arange(128)[None, None, :].astype(np.float64)
    ang = th * (32 * a + b) * (k1 + 0.5)
    M1 = np.stack([np.cos(ang), -np.sin(ang)], axis=2).reshape(128, 8192)
    bb = np.arange(32)[:, None].astype(np.float64)
    k2 = np.arange(32)[None, :].astype(np.float64)
    ph = 2 * np.pi * bb * k2 / 32
    I4 = np.eye(4)
    C2 = np.kron(np.cos(ph), I4)
    S2 = np.kron(np.sin(ph), I4)
    S2m = np.concatenate([C2, S2, -S2], axis=1)
    IC = np.kron(np.cos(ph).T, I4)
    IS = np.kron(np.sin(ph).T, I4)
    R12 = np.concatenate([IC, IS, -IS, IC], axis=1)
    k1c = np.arange(128)[:, None, None].astype(np.float64)
    bp = np.arange(32)[None, :, None].astype(np.float64)
    ap_ = np.arange(64)[None, None, :].astype(np.float64)
    ang2 = th * (32 * ap_ + bp) * (k1c + 0.5)
    IM = np.stack([(2.0 / N) * np.cos(ang2), -(2.0 / N) * np.sin(ang2)], axis=2).reshape(128, 4096)
    t = np.linspace(0.0, 1.0, L, dtype=np.float32)[:, None]
    bands = 16
    w = (np.float32(2.0 * math.pi / L) * np.arange(L, dtype=np.float32))[:, None]
    f = np.linspace(1e-4, bands - 1, bands, dtype=np.float32)[None, :]
    fw = (f * w).astype(np.float32)
    z = np.concatenate([t, np.cos(fw), -np.sin(fw)], axis=-1).astype(np.float32)
    negt = (-t[:, 0]).reshape(128, 32).astype(np.float32)
    return dict(M1=_bf(M1), S2m=_bf(S2m), R12=_bf(R12), IM=_bf(IM),
                ident=_bf(np.eye(128)), zT=np.ascontiguousarray(z.T), negt=np.ascontiguousarray(negt))


_CACHE = {}


def kernel(x, norm_g, w_in, conf_dw_w, conf_dw_b, conf_ln_g, conf_ln_b,
           hy_short_w, hy_short_b, filt_w1, filt_b1, filt_freq1,
           filt_w2, filt_b2, filt_freq2, filt_w3, filt_b3, filt_freq3,
           filt_w_out, hy_deltas, hy_skip, hy_norm_g, w_out, final_g):
    f32 = np.float32
    x = np.asarray(x, f32)
    if 'nc' not in _CACHE:
        _CACHE['nc'] = build_program()
        _CACHE['c'] = _consts()
    nc = _CACHE['nc']
    cs = _CACHE['c']

    def fm(v):
        return np.asarray(v, f32).reshape(8, 128).T

    common = dict(
        w_in=np.ascontiguousarray(np.asarray(w_in, f32)[0]),
        w_out=np.ascontiguousarray(np.asarray(w_out, f32)[0]),
        mlpw1=np.ascontiguousarray(np.asarray(filt_w1, f32)[0]),
        mlpw2=np.ascontiguousarray(np.asarray(filt_w2, f32)[0]),
        mlpw3=np.ascontiguousarray(np.asarray(filt_w3, f32)[0]),
        mlpp=np.ascontiguousarray(np.stack([np.asarray(v, f32)[0] for v in
                                            (filt_b1, filt_b2, filt_b3, filt_freq1, filt_freq2, filt_freq3)], axis=1)),
        skip=np.ascontiguousarray(np.asarray(hy_skip, f32).reshape(1, 1024)),
        final_g=np.ascontiguousarray(np.asarray(final_g, f32).reshape(1, 1024)),
        zT=cs['zT'], negt=cs['negt'], M1=cs['M1'], S2m=cs['S2m'], R12=cs['R12'], IM=cs['IM'], ident=cs['ident'],
    )
    in_maps = []
    for i in range(8):
        b, hf = i // 2, i % 2
        cw = np.asarray(conf_dw_w, f32)[0]
        sw = np.asarray(hy_short_w, f32)[0]
        wf = np.asarray(filt_w_out, f32)[0]
        dl = np.asarray(hy_deltas, f32)[0]
        xl = x[b]
        if hf == 1:
            xl = xl[::-1]
            cw = cw[::-1]
            sw = sw[::-1]
            wf = np.concatenate([wf[:, 1024:], wf[:, :1024]], axis=1)
            dl = dl[::-1]
        pvv = np.zeros((128, NPV), f32)
        pvv[:, 0:8] = fm(np.asarray(norm_g, f32)[0])
        pvv[:, 8:256] = cw.reshape(31, 8, 128).transpose(2, 1, 0).reshape(128, 248)
        pvv[:, 256:264] = fm(np.asarray(conf_dw_b, f32)[0])
        pvv[:, 264:272] = fm(np.asarray(conf_ln_g, f32)[0])
        pvv[:, 272:280] = fm(np.asarray(conf_ln_b, f32)[0])
        pvv[:, 280:352] = sw.reshape(3, 24, 128).transpose(2, 1, 0).reshape(128, 72)
        pvv[:, 352:376] = np.asarray(hy_short_b, f32)[0].reshape(24, 128).T
        pvv[:, 376:384] = fm(np.asarray(hy_norm_g, f32)[0])
        m = dict(common)
        m.update(x=np.ascontiguousarray(xl), pv=pvv, wf=np.ascontiguousarray(wf),
                 deltas=np.ascontiguousarray(dl))
        in_maps.append(m)
    res = run_bass_kernel_spmd(nc, in_maps, core_ids=list(range(8)))
    out = np.empty((4, L, D), f32)
    for i in range(8):
        b, hf = i // 2, i % 2
        o = np.asarray(res.results[i]["out"], f32)
        if hf == 0:
            out[b, :2048] = o
        else:
            out[b, 2048:] = o[::-1]
    return out
```

```python
import math
from contextlib import ExitStack

import numpy as np
import ml_dtypes

import concourse.bass as bass
import concourse.mybir as mybir
from concourse.ap import AP
from concourse.bass_utils import run_bass_kernel_spmd

F32 = mybir.dt.float32
BF16 = mybir.dt.bfloat16
ALU = mybir.AluOpType
AF = mybir.ActivationFunctionType

L = 4096
D = 1024
EPS = 1e-5
NPV = 384
EVAC_FORCE = None
ATTACH_WAIT = True
MAGIC = 12582912.0
TWO_PI = 2.0 * math.pi


class Sched:
    CHUNK = 3000

    def __init__(self):
        self.ops = []
        self.bufs = {}
        self.known = {}
        self.known_dma = {}
        self.dma_cnt = {}
        self.last_of = {}
        self.psum = set()
        self.dma_kvec = {}

    def add(self, eng, fn, reads=(), writes=(), dma=None):
        oid = len(self.ops)
        is_dma = dma is not None
        need_c = {}
        need_d = {}

        def consider(pid, raw):
            p = self.ops[pid]
            if p['dma'] is not None:
                k = p['dma']
                need_d[k] = max(need_d.get(k, 0), (1 << 30) if k == 'const' else p['dcount'])
                return
            if (not is_dma) and p['eng'] == eng:
                if eng == 'pe':
                    return
            need_c[p['eng']] = max(need_c.get(p['eng'], -1), pid)

        reads = [((n, 0, 1 << 30) if n in self.psum else (n, lo, hi)) for (n, lo, hi) in reads]
        writes = [((n, 0, 1 << 30) if n in self.psum else (n, lo, hi)) for (n, lo, hi) in writes]
        for (n, lo, hi) in reads:
            b = self.bufs.setdefault(n, {'w': [], 'r': []})
            for (l2, h2, pid) in b['w']:
                if l2 < hi and lo < h2:
                    consider(pid, True)
            if n in self.psum:
                for (l2, h2, pid) in b['r']:
                    consider(pid, False)
        for (n, lo, hi) in writes:
            b = self.bufs.setdefault(n, {'w': [], 'r': []})
            for (l2, h2, pid) in b['w']:
                if l2 < hi and lo < h2:
                    consider(pid, False)
            for (l2, h2, pid) in b['r']:
                if l2 < hi and lo < h2:
                    consider(pid, False)
        kn = self.known.setdefault(eng, {})
        kd = self.known_dma.setdefault(eng, {})
        wc = {}
        wd = {}

        def merge(kv):
            for e2, p2 in kv.items():
                if kn.get(e2, -1) < p2:
                    kn[e2] = p2

        for k, c in need_d.items():
            if kd.get(k, 0) < c:
                wd[k] = c
                kd[k] = c
                if k != 'const' and (k, c) in self.dma_kvec:
                    merge(self.dma_kvec[(k, c)])
        for e, pid in sorted(need_c.items(), key=lambda kv: -kv[1]):
            if kn.get(e, -1) < pid:
                wc[e] = pid
                kn[e] = pid
                self.ops[pid]['ms'] = True
                merge(self.ops[pid]['kvec'])
        op = dict(eng=eng, fn=fn, wc=wc, wd=wd, dma=dma, dcount=0, ms=False, kvec=dict(kn))
        if not is_dma:
            op['kvec'][eng] = max(op['kvec'].get(eng, -1), oid - 0)
        if is_dma:
            self.dma_cnt[dma] = self.dma_cnt.get(dma, 0) + 16
            op['dcount'] = self.dma_cnt[dma]
            self.dma_kvec[(dma, op['dcount'])] = dict(kn)
        self.ops.append(op)
        for (n, lo, hi) in writes:
            b = self.bufs[n]
            b['w'] = [r for r in b['w'] if not (lo <= r[0] and r[1] <= hi)]
            b['r'] = [r for r in b['r'] if not (lo <= r[0] and r[1] <= hi)]
            b['w'].append((lo, hi, oid))
        for (n, lo, hi) in reads:
            self.bufs[n]['r'].append((lo, hi, oid))
        return oid

    def emit(self, nc, final_waits):
        engs = ['pe', 'act', 'dve', 'pool', 'sp']
        msn = {}
        cnt = {e: 0 for e in engs}
        for i, op in enumerate(self.ops):
            if op['ms'] and op['dma'] is None:
                msn[i] = cnt[op['eng']]
                cnt[op['eng']] += 1
        with ExitStack() as st:
            csem = {}
            for e in engs:
                n = cnt[e] // self.CHUNK + 1
                csem[e] = [st.enter_context(nc.semaphore("s_%s_%d" % (e, j))) for j in range(n)]
            dsem = {k: st.enter_context(nc.semaphore("d_" + k)) for k in self.dma_cnt}
            block = st.enter_context(nc.Block())

            def run(ename, eobj):
                for i, op in enumerate(self.ops):
                    if op['eng'] != ename:
                        continue
                    waits = []
                    for e, pid in op['wc'].items():
                        m = msn[pid]
                        waits.append((csem[e][m // self.CHUNK], m % self.CHUNK + 1))
                    for k, c in op['wd'].items():
                        waits.append((dsem[k], self.dma_cnt[k] if k == 'const' else c))
                    attach = waits.pop() if (waits and ATTACH_WAIT) else None
                    for (sm_, v_) in waits:
                        eobj.wait_ge(sm_, v_)
                    ins = op['fn'](eobj)
                    if attach is not None:
                        ins._wait_ge(attach[0], attach[1])
                    if op['dma'] is not None:
                        ins.then_inc(dsem[op['dma']], 16)
                    elif op['ms']:
                        m = msn[i]
                        ins.then_inc(csem[ename][m // self.CHUNK], 1)
                if ename == 'sp':
                    for k in final_waits:
                        eobj.wait_ge(dsem[k], self.dma_cnt[k])

            @block.tensor
            def _(e):
                run('pe', e)

            @block.scalar
            def _(e):
                run('act', e)

            @block.vector
            def _(e):
                run('dve', e)

            @block.gpsimd
            def _(e):
                run('pool', e)

            @block.sync
            def _(e):
                run('sp', e)


class TV:
    def __init__(self, h, F, esz, base, coff=0, ncol=None):
        self.h = h
        self.F = F
        self.esz = esz
        self.base = base
        self.coff = coff
        self.ncol = F if ncol is None else ncol

    def ap(self, c0, dims, p0=0, npart=128):
        return AP(self.h, p0 * self.F + self.coff + c0, [[self.F, npart]] + [list(d) for d in dims])

    def r(self, c0=0, c1=None):
        if c1 is None:
            c1 = self.ncol
        return (self.base, (self.coff + c0) * self.esz, (self.coff + c1) * self.esz)

    def cast(self, dt, esz2):
        v = self.h[:, :].bitcast(dt)
        return TV(v.tensor, self.F * self.esz // esz2, esz2, self.base,
                  self.coff * self.esz // esz2, self.ncol * self.esz // esz2)

    def sub(self, c0, n):
        return TV(self.h, self.F, self.esz, self.base, self.coff + c0, n)


def build_program(PH='FXCH3', ncc=8, nch=8, dbg=False, hstop=99):
    nc = bass.Bass("TRN2", target_bir_lowering=False)
    S = Sched()

    def dram(name, shape, dt, kind="ExternalInput"):
        return nc.dram_tensor(name, shape, dt, kind=kind).ap()

    x_d = dram("x", [L, D], F32)
    win_d = dram("w_in", [D, 7168], F32)
    wout_d = dram("w_out", [2048, D], F32)
    pv_d = dram("pv", [128, NPV], F32)
    w1_d = dram("mlpw1", [33, 64], F32)
    w2_d = dram("mlpw2", [64, 64], F32)
    w3_d = dram("mlpw3", [64, 64], F32)
    mp_d = dram("mlpp", [64, 6], F32)
    wf_d = dram("wf", [64, 2048], F32)
    del_d = dram("deltas", [2, 1024], F32)
    skip_d = dram("skip", [1, 1024], F32)
    fg_d = dram("final_g", [1, 1024], F32)
    zT_d = dram("zT", [33, L], F32)
    negt_d = dram("negt", [128, 32], F32)
    M1_d = dram("M1", [128, 8192], BF16)
    S2_d = dram("S2m", [128, 384], BF16)
    R12_d = dram("R12", [128, 512], BF16)
    IM_d = dram("IM", [128, 4096], BF16)
    id_d = dram("ident", [128, 128], BF16)
    out_d = dram("out", [2048, D], F32, kind="ExternalOutput")
    mix_d = dram("mixd", [2048, 2048], BF16, kind=("ExternalOutput" if dbg else "Internal"))
    mix_w = mix_d.rearrange("(tt ml) (mc tok) -> ml tt mc tok", ml=128, tok=128)
    if dbg:
        dbg_hm = dram("dbg_hm", [64, L], BF16, kind="ExternalOutput")
        dbg_xn = dram("dbg_xn", [128, 8 * L], BF16, kind="ExternalOutput")

    with ExitStack() as st:
        def sb(name, F, dt):
            h = st.enter_context(nc.sbuf_tensor(name, [128, F], dt))
            return TV(h, F, 4 if dt == F32 else 2, name)

        def ps(name, F, dt):
            S.psum.add(name)
            h = st.enter_context(nc.psum_tensor(name, [128, F], dt))
            return TV(h, F, 4 if dt == F32 else 2, name)

        xnT = sb("xnT", 8 * L, BF16)
        M1 = sb("M1s", 8192, BF16)
        IM = sb("IMs", 4096, BF16)
        HmT = sb("HmT", L, BF16)
        Wf = sb("Wf", 2048, BF16)
        S2m = sb("S2ms", 384, BF16)
        R12 = sb("R12s", 512, BF16)
        ident = sb("idents", 128, BF16)
        onesm = sb("onesm", 128, F32)
        pv = sb("pvs", NPV, F32)
        negt = sb("negts", 32, F32)
        mlpw = sb("mlpw", 192, F32)
        mlpp = sb("mlpps", 8, F32)
        frb = sb("frb", 4, F32)
        wst = [sb("wst%d" % i, 1024, F32) for i in range(2)]
        wbf = [sb("wbf%d" % i, 1024, BF16) for i in range(2)]
        tmpall = sb("tmpall", 4096, F32)
        tmp = [tmpall.sub(512 * i, 512) for i in range(8)]
        small = [sb("small%d" % i, 8, F32) for i in range(4)]
        XTbuf = sb("XTbuf", 8192, BF16)
        Abuf = sb("Abuf", 4096, F32)
        Abf = Abuf.cast(BF16, 2)
        Kbuf = sb("Kbuf", 8192, BF16)
        raw = [sb("raw%d" % i, 1040, F32) for i in range(2)]
        accbuf = sb("accbuf", 2080, F32)
        a_pad = raw[0].cast(BF16, 2)
        pool4 = [sb("pl%d" % i, 1024, BF16) for i in range(4)]
        absd = sb("absd", 256, F32)
        skipb = sb("skipb", 128, F32)
        xt = [Kbuf.cast(F32, 4).sub(1024 * i, 1024) for i in range(2)]
        xnb = [XTbuf.sub(1024 * i, 1024) for i in range(2)]
        P = [ps("P%d" % i, 512, F32) for i in range(6)]
        Q = [ps("Q%d" % i, 1024, BF16) for i in range(2)]

        cnt = {'p': 0, 'q': 0, 't': 0, 's': 0, 'w': 0, 'pl': 0, 'x': 0, 'e': 0}

        def nextP():
            cnt['p'] += 1
            return P[cnt['p'] % 6]

        def nextQ():
            cnt['q'] += 1
            return Q[cnt['q'] % 2]

        def nextT():
            cnt['t'] += 1
            return tmp[cnt['t'] % 8]

        def nextS():
            cnt['s'] += 1
            return small[cnt['s'] % 4]

        def nextPl():
            cnt['pl'] += 1
            return pool4[cnt['pl'] % 4]

        def evac_eng():
            cnt['e'] += 1
            if EVAC_FORCE:
                return EVAC_FORCE
            return 'act' if cnt['e'] % 2 else 'dve'

        def copy_op(eng, out, in_, reads, writes):
            if eng == 'act':
                S.add('act', lambda e: e.activation(out, in_, AF.Copy), reads, writes)
            else:
                S.add(eng, lambda e: e.tensor_copy(out, in_), reads, writes)

        def cload(tv, c0, ncol, src, npart=128):
            S.add('sp', lambda e: e.dma_start(out=tv.ap(c0, [[1, ncol]], npart=npart), in_=src),
                  writes=[tv.r(c0, c0 + ncol)], dma='const')

        for i in range(4):
            cload(M1, 2048 * i, 2048, M1_d[:, 2048 * i:2048 * (i + 1)])
        for i in range(2):
            cload(IM, 2048 * i, 2048, IM_d[:, 2048 * i:2048 * (i + 1)])
        cload(S2m, 0, 384, S2_d[:, :])
        cload(R12, 0, 512, R12_d[:, :])
        cload(ident, 0, 128, id_d[:, :])
        cload(pv, 0, NPV, pv_d[:, :])
        cload(negt, 0, 32, negt_d[:, :])
        cload(mlpw, 0, 64, w1_d[:, :], npart=33)
        cload(mlpw, 64, 64, w2_d[:, :], npart=64)
        cload(mlpw, 128, 64, w3_d[:, :], npart=64)
        cload(mlpp, 0, 6, mp_d[:, :], npart=64)
        zT = Abuf
        for i in range(2):
            cload(zT, 2048 * i, 2048, zT_d[:, 2048 * i:2048 * (i + 1)], npart=33)
        for i in range(2):
            S.add('sp', lambda e, i=i: e.dma_start(out=xt[i].ap(0, [[1, 1024]], npart=64),
                                                   in_=wf_d[:, 1024 * i:1024 * (i + 1)]),
                  writes=[xt[i].r()], dma='xt%d' % i)
            S.add('dve', lambda e, i=i: e.tensor_copy(Wf.ap(1024 * i, [[1, 1024]], npart=64),
                                                      xt[i].ap(0, [[1, 1024]], npart=64)),
                  reads=[xt[i].r()], writes=[Wf.r(1024 * i, 1024 * (i + 1))])
        S.add('dve', lambda e: e.memset(onesm.ap(0, [[1, 128]]), 1.0 / 128.0), writes=[onesm.r()])
        epsb = sb("epsb", 2, F32)
        S.add('dve', lambda e: e.memset(epsb.ap(0, [[1, 2]]), EPS), writes=[epsb.r()])
        S.add('dve', lambda e: e.tensor_tensor(frb.ap(0, [[1, 3]], npart=64), mlpp.ap(0, [[1, 3]], npart=64),
                                               mlpp.ap(3, [[1, 3]], npart=64), ALU.mult),
              reads=[mlpp.r()], writes=[frb.r()])
        S.add('dve', lambda e: e.tensor_scalar(frb.ap(0, [[1, 3]], npart=64), frb.ap(0, [[1, 3]], npart=64),
                                               1.0 / TWO_PI, None, ALU.mult), reads=[frb.r()], writes=[frb.r()])
        S.add('dve', lambda e: e.tensor_scalar(mlpp.ap(3, [[1, 3]], npart=64), mlpp.ap(3, [[1, 3]], npart=64),
                                               1.0 / TWO_PI, None, ALU.mult), reads=[mlpp.r(), frb.r()], writes=[mlpp.r()])

        hA = Kbuf.cast(F32, 4)
        hB = XTbuf.cast(F32, 4)

        def mlp_layer(src_tv, src_np, wcol, li, dst_tv, dst_bf):
            for ti in range(8):
                c0 = 512 * ti
                pt = nextP()
                S.add('pe', lambda e, pt=pt, c0=c0: e.matmul(
                    pt.ap(0, [[1, 512]], npart=64), mlpw.ap(wcol, [[1, 64]], npart=src_np),
                    src_tv.ap(c0, [[1, 512]], npart=src_np), start=True, stop=True),
                    reads=[mlpw.r(wcol, wcol + 64), src_tv.r(c0, c0 + 512)], writes=[pt.r()])
                u = nextT()
                k = nextT()
                S.add('act', lambda e, pt=pt, u=u: e.activation(
                    u.ap(0, [[1, 512]], npart=64), pt.ap(0, [[1, 512]], npart=64), AF.Identity,
                    bias=frb.ap(li, [[1, 1]], npart=64), scale=mlpp.ap(3 + li, [[1, 1]], npart=64)),
                    reads=[pt.r(), mlpp.r(), frb.r()], writes=[u.r()])
                S.add('dve', lambda e, u=u, k=k: e.tensor_scalar(
                    k.ap(0, [[1, 512]], npart=64), u.ap(0, [[1, 512]], npart=64),
                    MAGIC, -MAGIC, ALU.add, ALU.add), reads=[u.r()], writes=[k.r()])
                S.add('dve', lambda e, u=u, k=k: e.tensor_tensor(
                    u.ap(0, [[1, 512]], npart=64), u.ap(0, [[1, 512]], npart=64),
                    k.ap(0, [[1, 512]], npart=64), ALU.subtract), reads=[u.r(), k.r()], writes=[u.r()])
                S.add('act', lambda e, u=u, c0=c0: e.activation(
                    dst_tv.ap(c0, [[1, 512]], npart=64), u.ap(0, [[1, 512]], npart=64), AF.Sin, scale=6.283185),
                    reads=[u.r()], writes=[dst_tv.r(c0, c0 + 512)])

        if 'F' in PH:
            mlp_layer(zT, 33, 0, 0, hA, False)
            mlp_layer(hA, 64, 64, 1, hB, False)
            mlp_layer(hB, 64, 128, 2, HmT, True)
        if dbg:
            S.add('sp', lambda e: e.dma_start(out=dbg_hm[:, :], in_=HmT.ap(0, [[1, L]], npart=64)),
                  reads=[HmT.r()], writes=[("dbg1", 0, 1)], dma='dbg')

        xt8 = [Kbuf.cast(F32, 4).sub(1024 * i, 1024) for i in range(4)] + [Abuf.sub(1024 * i, 1024) for i in range(4)]
        xnb8 = [XTbuf.sub(1024 * i, 1024) for i in range(8)]
        for g in (range(8) if 'X' in PH else []):
            sm = small[g % 4]
            for i in range(4):
                tt = 4 * g + i
                xs = xt8[(4 * g + i) % 8]
                S.add('sp', lambda e, xs=xs, tt=tt: e.dma_start(out=xs.ap(0, [[1, 1024]]),
                                                                in_=x_d[128 * tt:128 * (tt + 1), :]),
                      writes=[xs.r()], dma='xt8_%d' % ((4 * g + i) % 8))
            for i in range(4):
                xs = xt8[(4 * g + i) % 8]
                jt = nextT()
                S.add('act', lambda e, xs=xs, jt=jt, sm=sm, i=i: e.activation(
                    jt.cast(BF16, 2).ap(0, [[1, 1024]]), xs.ap(0, [[1, 1024]]), AF.Square,
                    accum_out=sm.ap(i, [[1, 1]])), reads=[xs.r()], writes=[jt.r(), sm.r(i, i + 1)])
            S.add('act', lambda e, sm=sm: e.activation(sm.ap(4, [[1, 4]]), sm.ap(0, [[1, 4]]), AF.Ln,
                                                       bias=epsb.ap(0, [[1, 1]]), scale=1.0 / D),
                  reads=[sm.r(0, 4), epsb.r()], writes=[sm.r(4, 8)])
            S.add('act', lambda e, sm=sm: e.activation(sm.ap(4, [[1, 4]]), sm.ap(4, [[1, 4]]), AF.Exp, scale=-0.5),
                  reads=[sm.r(4, 8)], writes=[sm.r(4, 8)])
            for i in range(4):
                tt = 4 * g + i
                xs = xt8[(4 * g + i) % 8]
                xb = xnb8[(4 * g + i) % 8]
                S.add('dve', lambda e, xs=xs, xb=xb, sm=sm, i=i: e.tensor_scalar(
                    xb.ap(0, [[1, 1024]]), xs.ap(0, [[1, 1024]]), sm.ap(4 + i, [[1, 1]]), None, ALU.mult),
                    reads=[xs.r(), sm.r(4 + i, 5 + i)], writes=[xb.r()])
                q = nextQ()
                for dc in range(8):
                    S.add('pe', lambda e, q=q, xb=xb, dc=dc: e.transpose(
                        q.ap(128 * dc, [[1, 128]]), xb.ap(128 * dc, [[1, 128]]), ident.ap(0, [[1, 128]])),
                        reads=[xb.r(128 * dc, 128 * dc + 128), ident.r()], writes=[q.r(128 * dc, 128 * dc + 128)])
                copy_op('dve', xnT.ap(128 * tt, [[L, 8], [1, 128]]), q.ap(0, [[128, 8], [1, 128]]),
                        [q.r()], [xnT.r(dc * L + 128 * tt, dc * L + 128 * tt + 128) for dc in range(8)])

        if dbg:
            for i in range(8):
                S.add('sp', lambda e, i=i: e.dma_start(out=dbg_xn[:, L * i:L * (i + 1)], in_=xnT.ap(L * i, [[1, L]])),
                      reads=[xnT.r(L * i, L * (i + 1))], writes=[("dbg2", i, i + 1)], dma='dbg')
        win_v = win_d.rearrange("(dc dl) e -> dl dc e", dl=128)

        wseq = []
        if 'C' in PH:
            wseq += [1024, 0]
            for cc_ in range(ncc):
                wseq += [2048 + 128 * cc_]
                if cc_ + 1 < ncc:
                    wseq += [1024 + 128 * (cc_ + 1), 128 * (cc_ + 1)]
        if 'H' in PH:
            for cc_ in range(nch):
                wseq += [4096 + 128 * cc_, 5120 + 128 * cc_, 3072 + 128 * cc_, 6144 + 128 * cc_]
        wstate = {'issued': 0, 'ptr': 0}

        def w_issue_upto(k):
            while wstate['issued'] < min(k, len(wseq)):
                i = wstate['issued']
                e0 = wseq[i]
                ws = wst[i % 2]
                S.add('sp', lambda e, ws=ws, e0=e0: e.dma_start(out=ws.ap(0, [[128, 8], [1, 128]]),
                                                              in_=win_v[:, :, e0:e0 + 128]),
                      writes=[ws.r()], dma=ws.base)
                wstate['issued'] += 1

        def load_w(e0):
            i = wstate['ptr']
            assert wseq[i] == e0, (i, wseq[i], e0)
            w_issue_upto(i + 2)
            ws, wb = wst[i % 2], wbf[i % 2]
            S.add('pool', lambda e: e.tensor_tensor(
                wb.ap(0, [[128, 8], [1, 128]]), ws.ap(0, [[128, 8], [1, 128]]),
                pv.ap(0, [[1, 8], [0, 128]]), ALU.mult), reads=[ws.r(), pv.r(0, 8)], writes=[wb.r()])
            wstate['ptr'] += 1
            return wb

        def inproj(wb, t0, n, pt):
            for dc in range(8):
                S.add('pe', lambda e, dc=dc: e.matmul(
                    pt.ap(0, [[1, n]]), wb.ap(128 * dc, [[1, 128]]), xnT.ap(dc * L + t0, [[1, n]]),
                    start=(dc == 0), stop=(dc == 7)),
                    reads=[wb.r(128 * dc, 128 * dc + 128), xnT.r(dc * L + t0, dc * L + t0 + n)],
                    writes=[pt.r(0, n)])

        def groups(lo, hi):
            out = []
            while lo < hi:
                n = min(512, hi - lo)
                out.append((lo, n))
                lo += n
            return out

        def groups_bal(lo, hi):
            n = hi - lo
            k = (n + 511) // 512
            base = n // k
            out = []
            for i in range(k):
                sz = base + (1 if i < n - base * k else 0)
                out.append((lo, sz))
                lo += sz
            return out

        def ln_tile(src_ap, src_reads, dst, n=512):
            S.add('act', lambda e: e.activation(dst.ap(0, [[1, n]]), src_ap, AF.Ln, bias=epsb.ap(0, [[1, 1]])),
                  reads=src_reads + [epsb.r()], writes=[dst.r(0, n)])

        def exph_tile(dst, n=512):
            S.add('act', lambda e: e.activation(dst.ap(0, [[1, n]]), dst.ap(0, [[1, n]]), AF.Exp, scale=-0.5),
                  reads=[dst.r(0, n)], writes=[dst.r(0, n)])


        sgb = accbuf
        dg = XTbuf
        S.add('dve', lambda e: e.memset(a_pad.ap(0, [[1, 16]]), 0.0), writes=[a_pad.r(0, 16)])
        Kf32 = Kbuf.cast(F32, 4)
        c_ac, c_dd = Abuf.sub(0, 2048), Abuf.sub(2048, 2048)
        c_sq, c_sz = Kf32.sub(0, 2048), Kf32.sub(2048, 2048)
        conf_chunks = list(range(ncc)) if 'C' in PH else []

        def conf_prep_dg(cc):
            S.add('dve', lambda e, cc=cc: e.tensor_tensor(
                dg.ap(0, [[128, 31], [1, 128]]), ident.ap(0, [[0, 31], [1, 128]]),
                pv.ap(8 + cc * 31, [[1, 31], [0, 128]]), ALU.mult),
                reads=[ident.r(), pv.r(8 + cc * 31, 8 + cc * 31 + 31)], writes=[dg.r(0, 31 * 128)])

        def conf_prep_gate(cc):
            wg = load_w(1024 + 128 * cc)
            for (t0, n) in groups_bal(0, 2063):
                pt = nextP()
                inproj(wg, t0, n, pt)
                S.add('act', lambda e, pt=pt, t0=t0, n=n: e.activation(
                    sgb.ap(t0, [[1, n]]), pt.ap(0, [[1, n]]), AF.Sigmoid),
                    reads=[pt.r(0, n)], writes=[sgb.r(t0, t0 + n)])

        def conf_prep_val(cc):
            wv = load_w(128 * cc)
            for (t0, n) in groups_bal(0, 2063):
                pt = nextP()
                inproj(wv, t0, n, pt)
                S.add('dve', lambda e, pt=pt, t0=t0, n=n: e.tensor_tensor(
                    a_pad.ap(15 + t0, [[1, n]]), pt.ap(0, [[1, n]]), sgb.ap(t0, [[1, n]]), ALU.mult),
                    reads=[pt.r(0, n), sgb.r(t0, t0 + n)], writes=[a_pad.r(15 + t0, 15 + t0 + n)])

        if conf_chunks:
            conf_prep_dg(0)
            conf_prep_gate(0)
            conf_prep_val(0)
        for ci, cc in enumerate(conf_chunks):
            nxt = conf_chunks[ci + 1] if ci + 1 < len(conf_chunks) else None
            wz = load_w(2048 + 128 * cc)
            for ti in range(4):
                t0 = 512 * ti
                pc = nextP()
                for k in range(31):
                    S.add('pe', lambda e, pc=pc, k=k, t0=t0: e.matmul(
                        pc.ap(0, [[1, 512]]), dg.ap(128 * k, [[1, 128]]), a_pad.ap(t0 + k, [[1, 512]]),
                        start=(k == 0), stop=(k == 30)),
                        reads=[dg.r(128 * k, 128 * k + 128), a_pad.r(t0 + k, t0 + k + 512)], writes=[pc.r()])
                S.add('act', lambda e, pc=pc, t0=t0, cc=cc: e.activation(
                    c_ac.ap(t0, [[1, 512]]), pc.ap(0, [[1, 512]]), AF.Identity, bias=pv.ap(256 + cc, [[1, 1]])),
                    reads=[pc.r(), pv.r(256 + cc, 257 + cc)], writes=[c_ac.r(t0, t0 + 512)])
            for ti in range(4):
                t0 = 512 * ti
                pm = nextP()
                S.add('pe', lambda e, pm=pm, t0=t0: e.matmul(pm.ap(0, [[1, 512]]), onesm.ap(0, [[1, 128]]),
                                                            c_ac.ap(t0, [[1, 512]]), start=True, stop=True),
                      reads=[onesm.r(), c_ac.r(t0, t0 + 512)], writes=[pm.r()])
                S.add('dve', lambda e, pm=pm, t0=t0: e.tensor_tensor(
                    c_dd.ap(t0, [[1, 512]]), c_ac.ap(t0, [[1, 512]]), pm.ap(0, [[1, 512]]), ALU.subtract),
                    reads=[c_ac.r(t0, t0 + 512), pm.r()], writes=[c_dd.r(t0, t0 + 512)])
                S.add('act', lambda e, t0=t0: e.activation(c_sq.ap(t0, [[1, 512]]), c_dd.ap(t0, [[1, 512]]), AF.Square),
                      reads=[c_dd.r(t0, t0 + 512)], writes=[c_sq.r(t0, t0 + 512)])
            if nxt is not None:
                conf_prep_dg(nxt)
                conf_prep_gate(nxt)
            for ti in range(4):
                t0 = 512 * ti
                pz = nextP()
                inproj(wz, t0, 512, pz)
                S.add('act', lambda e, pz=pz, t0=t0: e.activation(c_sz.ap(t0, [[1, 512]]), pz.ap(0, [[1, 512]]), AF.Silu),
                      reads=[pz.r()], writes=[c_sz.r(t0, t0 + 512)])
            for ti in range(4):
                t0 = 512 * ti
                pvv = nextP()
                S.add('pe', lambda e, pvv=pvv, t0=t0: e.matmul(pvv.ap(0, [[1, 512]]), onesm.ap(0, [[1, 128]]),
                                                              c_sq.ap(t0, [[1, 512]]), start=True, stop=True),
                      reads=[onesm.r(), c_sq.r(t0, t0 + 512)], writes=[pvv.r()])
                ln_tile(pvv.ap(0, [[1, 512]]), [pvv.r()], c_sq.sub(t0, 512))
            for ti in range(4):
                t0 = 512 * ti
                exph_tile(c_sq.sub(t0, 512))
                S.add('dve', lambda e, t0=t0: e.tensor_tensor(
                    c_dd.ap(t0, [[1, 512]]), c_dd.ap(t0, [[1, 512]]), c_sq.ap(t0, [[1, 512]]), ALU.mult),
                    reads=[c_dd.r(t0, t0 + 512), c_sq.r(t0, t0 + 512)], writes=[c_dd.r(t0, t0 + 512)])
            for ti in range(4):
                t0 = 512 * ti
                S.add('act', lambda e, t0=t0, cc=cc: e.activation(
                    c_dd.ap(t0, [[1, 512]]), c_dd.ap(t0, [[1, 512]]), AF.Silu,
                    bias=pv.ap(272 + cc, [[1, 1]]), scale=pv.ap(264 + cc, [[1, 1]])),
                    reads=[c_dd.r(t0, t0 + 512), pv.r(264 + cc, 273 + cc)], writes=[c_dd.r(t0, t0 + 512)])
            if nxt is not None:
                conf_prep_val(nxt)
            for ti in range(4):
                t0 = 512 * ti
                mo = nextPl()
                S.add('dve', lambda e, t0=t0, mo=mo: e.tensor_tensor(
                    mo.ap(0, [[1, 512]]), c_dd.ap(t0, [[1, 512]]), c_sz.ap(t0, [[1, 512]]), ALU.mult),
                    reads=[c_dd.r(t0, t0 + 512), c_sz.r(t0, t0 + 512)], writes=[mo.r(0, 512)])
                S.add('sp', lambda e, mo=mo, cc=cc, ti=ti: e.dma_start(
                    out=mix_w[:, 4 * ti:4 * ti + 4, cc, :], in_=mo.ap(0, [[128, 4], [1, 128]])),
                    reads=[mo.r(0, 512)], writes=[("mixd", 0, 1)], dma='mw_' + mo.base)

        XTs = XTbuf
        XTd_off = 4096
        Kre_off, Kim_off = 0, 4096
        hx0 = accbuf
        accv_off, accx_off = 0, 1024
        yb = accbuf

        raw4 = [raw[0], raw[1], accbuf.sub(0, 1040), accbuf.sub(1040, 1040)]

        rawb = [raw[0].cast(BF16, 2).sub(0, 1040), raw[0].cast(BF16, 2).sub(1040, 1040),
                raw[1].cast(BF16, 2).sub(0, 1040), raw[1].cast(BF16, 2).sub(1040, 1040)]
        dg3 = [mlpw.cast(BF16, 2), sb("dg3b", 384, BF16)]

        def build_dg3(slot, j):
            wc = 280 + 3 * j
            S.add('dve', lambda e: e.tensor_tensor(
                dg3[slot].ap(0, [[128, 3], [1, 128]]), ident.ap(0, [[0, 3], [1, 128]]),
                pv.ap(wc, [[1, 3], [0, 128]]), ALU.mult),
                reads=[ident.r(), pv.r(wc, wc + 3)], writes=[dg3[slot].r(0, 384)])

        def shortconv_quarter(wb, j, q, dst_tv, dst_off, dslot):
            cnt['x'] += 1
            rw = rawb[cnt['x'] % 4]
            jlo, jhi = 0, 1026
            if q == 0:
                jlo = 1
                S.add('dve', lambda e: e.memset(rw.ap(0, [[1, 1]]), 0.0), writes=[rw.r(0, 1)])
            if q == 3:
                jhi = 1025
                S.add('dve', lambda e: e.memset(rw.ap(1025, [[1, 1]]), 0.0), writes=[rw.r(1025, 1026)])
            bc = 352 + j
            for (j0, n) in groups_bal(jlo, jhi):
                t0 = 1024 * q - 1 + j0
                pt = nextP()
                inproj(wb, t0, n, pt)
                S.add('act', lambda e, pt=pt, j0=j0, n=n: e.activation(rw.ap(j0, [[1, n]]), pt.ap(0, [[1, n]]), AF.Copy),
                      reads=[pt.r(0, n)], writes=[rw.r(j0, j0 + n)])
                yield ('sc', j0)
            dg = dg3[dslot]
            for h in range(2):
                pc = nextP()
                for k in range(3):
                    S.add('pe', lambda e, pc=pc, k=k, h=h: e.matmul(
                        pc.ap(0, [[1, 512]]), dg.ap(128 * k, [[1, 128]]), rw.ap(512 * h + k, [[1, 512]]),
                        start=(k == 0), stop=(k == 2)),
                        reads=[dg.r(128 * k, 128 * k + 128), rw.r(512 * h + k, 512 * h + k + 512)], writes=[pc.r()])
                S.add('act', lambda e, pc=pc, h=h: e.activation(
                    dst_tv.ap(dst_off + 512 * h, [[1, 512]]), pc.ap(0, [[1, 512]]), AF.Identity,
                    bias=pv.ap(bc, [[1, 1]])),
                    reads=[pc.r(), pv.r(bc, bc + 1)],
                    writes=[dst_tv.r(dst_off + 512 * h, dst_off + 512 * h + 512)])
            yield ('tap', q)

        def transform(xt_off, consumer, need='both', bg_eng=None):
            for bp in range(16):
                pt = nextP()
                for bb in range(2):
                    b = 2 * bp + bb
                    S.add('pe', lambda e, pt=pt, b=b, bb=bb: e.matmul(
                        pt.ap(256 * bb, [[1, 256]]), XTbuf.ap(xt_off + 128 * b, [[1, 128]]),
                        M1.ap(256 * b, [[1, 256]]), start=True, stop=True),
                        reads=[XTbuf.r(xt_off + 128 * b, xt_off + 128 * b + 128), M1.r(256 * b, 256 * b + 256)],
                        writes=[pt.r(256 * bb, 256 * bb + 256)])
                for bb in range(2):
                    b = 2 * bp + bb
                    copy_op('dve', Abf.ap(4 * b, [[4096, 2], [128, 32], [1, 4]]),
                            pt.ap(256 * bb, [[128, 2], [4, 32], [1, 4]]),
                            [pt.r(256 * bb, 256 * bb + 256)], [Abf.r(0, 8192)])
                yield ('s1', bp)
            def stage_T(g):
                q = nextQ()
                for ri in range(2):
                    for j in range(4):
                        k1hi = 4 * g + j
                        col = (ri * 4 + j) * 128
                        S.add('pe', lambda e, q=q, ri=ri, k1hi=k1hi, col=col: e.transpose(
                            q.ap(col, [[1, 128]]), Abf.ap(ri * 4096 + 128 * k1hi, [[1, 128]]),
                            ident.ap(0, [[1, 128]])),
                            reads=[Abf.r(0, 8192), ident.r()], writes=[q.r(col, col + 128)])
                bg = nextPl()
                copy_op(bg_eng or evac_eng(), bg.ap(0, [[1, 1024]]), q.ap(0, [[1, 1024]]), [q.r()], [bg.r()])
                return bg

            def stage_S(g, bg):
                pre = nextP() if need in ('both', 're') else None
                pim = nextP() if need in ('both', 'im') else None
                mm = []
                if pre is not None:
                    mm += [(pre, 0, 0, True), (pre, 128, 512, False)]
                if pim is not None:
                    mm += [(pim, 0, 512, True), (pim, 256, 0, False)]
                for (po, mcol, bcol, stt) in mm:
                    S.add('pe', lambda e, po=po, mcol=mcol, bcol=bcol, stt=stt, bg=bg: e.matmul(
                        po.ap(0, [[1, 512]]), S2m.ap(mcol, [[1, 128]]), bg.ap(bcol, [[1, 512]]),
                        start=stt, stop=(not stt)),
                        reads=[S2m.r(), bg.r(bcol, bcol + 512)], writes=[po.r()])
                return consumer(g, pre, pim)

            bgs = {0: stage_T(0)}
            pend = None
            for g in range(8):
                if g + 1 < 8:
                    bgs[g + 1] = stage_T(g + 1)
                d = stage_S(g, bgs.pop(g))
                if pend is not None:
                    pend()
                pend = d
                yield ('pc', g)
            if pend is not None:
                pend()

        hy_chunks = list(range(nch)) if 'H' in PH else []

        def load_absd(cc):
            for d_ in range(2):
                S.add('sp', lambda e, d_=d_, cc=cc: e.dma_start(
                    out=absd.ap(128 * d_, [[128, 1], [1, 128]]),
                    in_=del_d[d_:d_ + 1, 128 * cc:128 * cc + 128].partition_broadcast(128)),
                    writes=[absd.r(128 * d_, 128 * d_ + 128)], dma='absd')
            S.add('act', lambda e: e.activation(absd.ap(0, [[1, 256]]), absd.ap(0, [[1, 256]]), AF.Abs),
                  reads=[absd.r()], writes=[absd.r()])

        def load_skipb(cc):
            S.add('sp', lambda e, cc=cc: e.dma_start(
                out=skipb.ap(0, [[128, 1], [1, 128]]), in_=skip_d[0:1, 128 * cc:128 * cc + 128].partition_broadcast(128)),
                writes=[skipb.r()], dma='skipb')

        def filter_evac_gen(cc):
            for gb in range(8):
                ph = [nextP(), nextP()]
                for d_ in range(2):
                    for j in range(4):
                        b = 4 * gb + j
                        S.add('pe', lambda e, d_=d_, j=j, b=b, cc=cc, ph=ph: e.matmul(
                            ph[d_].ap(128 * j, [[1, 128]]), HmT.ap(b, [[32, 128]], npart=64),
                            Wf.ap(1024 * d_ + 128 * cc, [[1, 128]], npart=64), start=True, stop=True),
                            reads=[HmT.r(), Wf.r(1024 * d_ + 128 * cc, 1024 * d_ + 128 * cc + 128)],
                            writes=[ph[d_].r(128 * j, 128 * j + 128)])
                cnt['dp'] = cnt.get('dp', 0) + 1
                dec2 = tmpall.sub(1024 * (cnt['dp'] % 4), 1024)
                for j in range(4):
                    b = 4 * gb + j
                    S.add('act', lambda e, dec2=dec2, j=j, b=b: e.activation(
                        dec2.ap(256 * j, [[1, 256]]), absd.ap(0, [[1, 256]]), AF.Exp,
                        scale=negt.ap(b, [[1, 1]])),
                        reads=[negt.r(), absd.r(0, 256)], writes=[dec2.r(256 * j, 256 * j + 256)])
                for d_ in range(2):
                    S.add('dve', lambda e, dec2=dec2, d_=d_, ph=ph: e.tensor_tensor(
                        dec2.ap(128 * d_, [[256, 4], [1, 128]]), ph[d_].ap(0, [[128, 4], [1, 128]]),
                        dec2.ap(128 * d_, [[256, 4], [1, 128]]), ALU.mult),
                        reads=[ph[d_].r(), dec2.r()], writes=[dec2.r()])
                S.add('dve', lambda e, dec2=dec2, gb=gb: e.tensor_tensor(
                    XTbuf.ap(512 * gb, [[128, 4], [1, 128]]), dec2.ap(0, [[256, 4], [1, 128]]),
                    dec2.ap(128, [[256, 4], [1, 128]]), ALU.add),
                    reads=[dec2.r()], writes=[XTbuf.r(512 * gb, 512 * gb + 512)])
                S.add('dve', lambda e, dec2=dec2, gb=gb: e.tensor_tensor(
                    XTbuf.ap(XTd_off + 512 * gb, [[128, 4], [1, 128]]), dec2.ap(0, [[256, 4], [1, 128]]),
                    dec2.ap(128, [[256, 4], [1, 128]]), ALU.subtract),
                    reads=[dec2.r()], writes=[XTbuf.r(XTd_off + 512 * gb, XTd_off + 512 * gb + 512)])
                if gb == 0:
                    S.add('dve', lambda e: e.tensor_tensor(
                        XTbuf.ap(0, [[1, 128]], npart=1), XTbuf.ap(0, [[1, 128]], npart=1),
                        skipb.ap(0, [[1, 128]], npart=1), ALU.add),
                        reads=[XTbuf.r(0, 128), skipb.r()], writes=[XTbuf.r(0, 128)])
                yield gb
            if cc + 1 < len(hy_chunks):
                load_absd(cc + 1)
                load_skipb(cc + 1)

        if hy_chunks:
            load_absd(0)
            load_skipb(0)
            for _ in filter_evac_gen(0):
                pass
        for cc in hy_chunks:
            if hstop <= 2:
                continue
            def cons_s(g, pre, pim):
                copy_op('act' if g % 2 else 'dve', Kbuf.ap(Kre_off + 512 * g, [[1, 512]]), pre.ap(0, [[1, 512]]),
                        [pre.r()], [Kbuf.r(Kre_off + 512 * g, Kre_off + 512 * g + 512)])

            def cons_d(g, pre, pim):
                copy_op('act', Kbuf.ap(Kim_off + 512 * g, [[1, 512]]), pim.ap(0, [[1, 512]]),
                        [pim.r()], [Kbuf.r(Kim_off + 512 * g, Kim_off + 512 * g + 512)])

            raw2 = [raw[0], raw[1]]

            def vx1_gen():
                for q4 in range(4):
                    yield from shortconv_quarter(wx1, 8 + cc, q4, accbuf, 1040, 0)
                    yield from shortconv_quarter(wvv, 16 + cc, q4, accbuf, 0, 1)
                    S.add('dve', lambda e, q4=q4: e.tensor_tensor(
                        XTbuf.ap(1024 * q4, [[1, 1024]]), accbuf.ap(0, [[1, 1024]]),
                        accbuf.ap(1040, [[1, 1024]]), ALU.mult),
                        reads=[accbuf.r(0, 1024), accbuf.r(1040, 2064)],
                        writes=[XTbuf.r(1024 * q4, 1024 * q4 + 1024)])
                    yield ('w', q4)

            gs = transform(0, cons_s, 're')
            for _ in range(10):
                next(gs)
            wx1 = load_w(4096 + 128 * cc)
            wvv = load_w(5120 + 128 * cc)
            build_dg3(0, 8 + cc)
            build_dg3(1, 16 + cc)
            gv = vx1_gen()

            def chain2():
                yield from gs
                yield from transform(XTd_off, cons_d, 'im')

            gf = chain2()
            while True:
                a_ = next(gf, None)
                b_ = next(gv, None)
                if a_ is None and b_ is None:
                    break

            if hstop <= 3:
                continue
            if hstop <= 4:
                continue
            for gq in range(4):
                q = nextQ()
                for j in range(8):
                    b = 8 * gq + j
                    S.add('pe', lambda e, q=q, j=j, b=b: e.transpose(
                        q.ap(128 * j, [[1, 128]]), XTbuf.ap(b, [[32, 128]]), ident.ap(0, [[1, 128]])),
                        reads=[XTbuf.r(0, 4096), ident.r()], writes=[q.r(128 * j, 128 * j + 128)])
                copy_op('dve', XTbuf.ap(XTd_off + 1024 * gq, [[1, 1024]]), q.ap(0, [[1, 1024]]),
                        [q.r()], [XTbuf.r(XTd_off + 1024 * gq, XTd_off + 1024 * gq + 1024)])

            if hstop <= 5:
                continue

            def cons_x(g, pre, pim):
                kre = Kbuf.ap(Kre_off + 512 * g, [[1, 512]])
                kim = Kbuf.ap(Kim_off + 512 * g, [[1, 512]])
                kr_r = Kbuf.r(Kre_off + 512 * g, Kre_off + 512 * g + 512)
                ki_r = Kbuf.r(Kim_off + 512 * g, Kim_off + 512 * g + 512)
                yg = nextPl()
                xr = pre
                xi = pim
                t1, t2 = nextT(), nextT()
                t3, t4 = nextT(), nextT()
                S.add('dve', lambda e: e.tensor_tensor(t1.ap(0, [[1, 512]]), xr.ap(0, [[1, 512]]), kre, ALU.mult),
                      reads=[xr.r(), kr_r], writes=[t1.r()])
                S.add('dve', lambda e: e.tensor_tensor(t3.ap(0, [[1, 512]]), xr.ap(0, [[1, 512]]), kim, ALU.mult),
                      reads=[xr.r(), ki_r], writes=[t3.r()])
                S.add('dve', lambda e: e.tensor_tensor(t2.ap(0, [[1, 512]]), xi.ap(0, [[1, 512]]), kim, ALU.mult),
                      reads=[xi.r(), ki_r], writes=[t2.r()])
                S.add('dve', lambda e: e.tensor_tensor(t4.ap(0, [[1, 512]]), xi.ap(0, [[1, 512]]), kre, ALU.mult),
                      reads=[xi.r(), kr_r], writes=[t4.r()])
                S.add('dve', lambda e: e.tensor_tensor(yg.ap(0, [[1, 512]]), t1.ap(0, [[1, 512]]),
                                                       t2.ap(0, [[1, 512]]), ALU.subtract),
                      reads=[t1.r(), t2.r()], writes=[yg.r(0, 512)])
                S.add('pool', lambda e: e.tensor_tensor(yg.ap(512, [[1, 512]]), t3.ap(0, [[1, 512]]),
                                                        t4.ap(0, [[1, 512]]), ALU.add),
                      reads=[t3.r(), t4.r()], writes=[yg.r(512, 1024)])
                def deferred():
                    for jp in range(2):
                        pt = nextP()
                        for jj in range(2):
                            j = 2 * jp + jj
                            S.add('pe', lambda e, pt=pt, jj=jj, j=j: e.matmul(
                                pt.ap(256 * jj, [[1, 256]]), yg.ap(128 * j, [[1, 128]]), R12.ap(0, [[1, 256]]),
                                start=True, stop=False),
                                reads=[yg.r(128 * j, 128 * j + 128), R12.r()],
                                writes=[pt.r(256 * jj, 256 * jj + 256)])
                            S.add('pe', lambda e, pt=pt, jj=jj, j=j: e.matmul(
                                pt.ap(256 * jj, [[1, 256]]), yg.ap(512 + 128 * j, [[1, 128]]), R12.ap(256, [[1, 256]]),
                                start=False, stop=True),
                                reads=[yg.r(512 + 128 * j, 512 + 128 * j + 128), R12.r()],
                                writes=[pt.r(256 * jj, 256 * jj + 256)])
                        for jj in range(2):
                            k1hi = 4 * g + 2 * jp + jj
                            copy_op('act', XTbuf.ap(4 * k1hi, [[128, 2], [256, 32], [1, 4]]),
                                    pt.ap(256 * jj, [[128, 2], [4, 32], [1, 4]]),
                                    [pt.r(256 * jj, 256 * jj + 256)], [XTbuf.r(0, 8192)])
                return deferred

            wx0 = load_w(3072 + 128 * cc)

            def x0_gen():
                for q4 in range(2):
                    yield from shortconv_quarter(wx0, cc, q4, hx0, 1024 * q4, 0)

            build_dg3(0, cc)
            gx = transform(XTd_off, cons_x, 'both', 'act')
            g0 = x0_gen()
            for _ in range(16):
                next(gx)
            while True:
                a_ = next(gx, None)
                b_ = next(g0, None)
                if a_ is None and b_ is None:
                    break
            Dsb = XTbuf

            if hstop <= 6:
                continue
            if hstop <= 7:
                continue
            PY = [P[0], P[1], P[2], P[3]]

            def it1_T(gb):
                q = nextQ()
                for j in range(4):
                    bp_ = 4 * gb + j
                    for ri in range(2):
                        col = (2 * j + ri) * 128
                        S.add('pe', lambda e, q=q, col=col, ri=ri, bp_=bp_: e.transpose(
                            q.ap(col, [[1, 128]]), Dsb.ap(bp_ * 256 + ri * 128, [[1, 128]]),
                            ident.ap(0, [[1, 128]])),
                            reads=[Dsb.r(0, 8192), ident.r()], writes=[q.r(col, col + 128)])
                dt_ = nextPl()
                copy_op('dve', dt_.ap(0, [[1, 1024]]), q.ap(0, [[1, 1024]]), [q.r()], [dt_.r()])
                return dt_

            def is1_S(gb, dt_):
                for j in range(4):
                    bp_ = 4 * gb + j
                    pyb = PY[bp_ // 8]
                    oc = (bp_ % 8) * 64
                    for ri in range(2):
                        S.add('pe', lambda e, pyb=pyb, oc=oc, j=j, ri=ri, bp_=bp_, dt_=dt_: e.matmul(
                            pyb.ap(oc, [[1, 64]]), dt_.ap((2 * j + ri) * 128, [[1, 128]]),
                            IM.ap((bp_ * 2 + ri) * 64, [[1, 64]]), start=(ri == 0), stop=(ri == 1)),
                            reads=[dt_.r((2 * j + ri) * 128, (2 * j + ri) * 128 + 128),
                                   IM.r((bp_ * 2 + ri) * 64, (bp_ * 2 + ri) * 64 + 64)],
                            writes=[pyb.r(oc, oc + 64)])

            dts = {0: it1_T(0)}
            for gb in range(8):
                if gb + 1 < 8:
                    dts[gb + 1] = it1_T(gb + 1)
                is1_S(gb, dts.pop(gb))
            if hstop <= 8:
                continue
            for kb in range(4):
                S.add('dve', lambda e, kb=kb: e.tensor_tensor(
                    yb.ap(8 * kb, [[32, 64], [1, 8]]), PY[kb].ap(0, [[1, 64], [64, 8]]),
                    hx0.ap(8 * kb, [[32, 64], [1, 8]]), ALU.mult),
                    reads=[PY[kb].r(), hx0.r(0, 2048)], writes=[yb.r(0, 2048)])
            if hstop <= 9:
                continue
            wz = load_w(6144 + 128 * cc)
            Kf = Kbuf.cast(F32, 4)
            fsq = [Kf.sub(512 * i, 512) for i in range(4)]
            fsz = [Abuf.sub(2048 + 512 * i, 512) for i in range(4)]
            fgen = filter_evac_gen(cc + 1) if cc + 1 < len(hy_chunks) else iter(())

            def adv(k):
                for _ in range(k):
                    next(fgen, None)

            for ti in range(4):
                t0 = 512 * ti
                S.add('act', lambda e, ti=ti, t0=t0: e.activation(fsq[ti].ap(0, [[1, 512]]), yb.ap(t0, [[1, 512]]), AF.Square),
                      reads=[yb.r(t0, t0 + 512)], writes=[fsq[ti].r()])
            adv(2)
            for ti in range(4):
                pvv = nextP()
                S.add('pe', lambda e, pvv=pvv, ti=ti: e.matmul(pvv.ap(0, [[1, 512]]), onesm.ap(0, [[1, 128]]),
                                                              fsq[ti].ap(0, [[1, 512]]), start=True, stop=True),
                      reads=[onesm.r(), fsq[ti].r()], writes=[pvv.r()])
                ln_tile(pvv.ap(0, [[1, 512]]), [pvv.r()], fsq[ti])
            adv(2)
            for ti in range(4):
                t0 = 512 * ti
                exph_tile(fsq[ti])
                S.add('dve', lambda e, ti=ti, t0=t0: e.tensor_tensor(
                    fsq[ti].ap(0, [[1, 512]]), fsq[ti].ap(0, [[1, 512]]), yb.ap(t0, [[1, 512]]), ALU.mult),
                    reads=[fsq[ti].r(), yb.r(t0, t0 + 512)], writes=[fsq[ti].r()])
            adv(2)
            for ti in range(4):
                t0 = 512 * ti
                pz = nextP()
                inproj(wz, t0, 512, pz)
                S.add('act', lambda e, pz=pz, ti=ti: e.activation(fsz[ti].ap(0, [[1, 512]]), pz.ap(0, [[1, 512]]), AF.Silu),
                      reads=[pz.r()], writes=[fsz[ti].r()])
            adv(1)
            for ti in range(4):
                t0 = 512 * ti
                mo = nextPl()
                S.add('dve', lambda e, ti=ti, mo=mo, cc=cc: e.scalar_tensor_tensor(
                    mo.ap(0, [[1, 512]]), fsq[ti].ap(0, [[1, 512]]), pv.ap(376 + cc, [[1, 1]]), fsz[ti].ap(0, [[1, 512]]),
                    ALU.mult, ALU.mult),
                    reads=[fsq[ti].r(), fsz[ti].r(), pv.r(376 + cc, 377 + cc)], writes=[mo.r(0, 512)])
                S.add('sp', lambda e, mo=mo, cc=cc, ti=ti: e.dma_start(
                    out=mix_w[:, 4 * ti:4 * ti + 4, 8 + cc, :], in_=mo.ap(0, [[128, 4], [1, 128]])),
                    reads=[mo.r(0, 512)], writes=[("mixd", 0, 1)], dma='mw_' + mo.base)
            adv(8)

        Wo = xnT
        wout_v = wout_d.rearrange("(mc ml) d -> ml mc d", ml=128)
        for mc in (range(16) if '3' in PH else []):
            ws = xt[mc % 2]
            S.add('sp', lambda e, ws=ws, mc=mc: e.dma_start(out=ws.ap(0, [[1, 1024]]), in_=wout_v[:, mc, :]),
                  writes=[ws.r()], dma='xt%d' % (mc % 2))
            copy_op('dve', Wo.ap(1024 * mc, [[1, 1024]]), ws.ap(0, [[1, 1024]]),
                    [ws.r()], [Wo.r(1024 * mc, 1024 * mc + 1024)])
        fgb = Kbuf.cast(F32, 4).sub(2048, 1024)
        S.add('sp', lambda e: e.dma_start(out=fgb.ap(0, [[1024, 1], [1, 1024]]), in_=fg_d[0:1, :].partition_broadcast(128)),
              writes=[fgb.r(0, 1024)], dma='fgb')

        hsb = Abuf
        xres = [accbuf.sub(0, 1024), accbuf.sub(1040, 1024), raw[0].sub(0, 1024), raw[1].sub(0, 1024)]
        for tt in (range(16) if '3' in PH else []):
            mt = XTbuf
            mo_ = (tt % 4) * 2048
            S.add('sp', lambda e, tt=tt, mo_=mo_: e.dma_start(
                out=mt.ap(mo_, [[1, 2048]]), in_=mix_d[128 * tt:128 * (tt + 1), :]),
                reads=[("mixd", 0, 1)], writes=[mt.r(mo_, mo_ + 2048)], dma='mixr%d' % (tt % 4))
            xr_ = xres[tt % 4]
            S.add('sp', lambda e, tt=tt, xr_=xr_: e.dma_start(out=xr_.ap(0, [[1, 1024]]),
                                                             in_=x_d[128 * tt:128 * (tt + 1), :]),
                  writes=[xr_.r()], dma='xres%d' % (tt % 4))
            ho = (tt % 2) * 2048
            for dh in range(2):
                pt = nextP()
                for mc in range(16):
                    S.add('pe', lambda e, pt=pt, mc=mc, dh=dh, mo_=mo_: e.matmul(
                        pt.ap(0, [[1, 512]]), mt.ap(mo_ + 128 * mc, [[1, 128]]),
                        Wo.ap(1024 * mc + 512 * dh, [[1, 512]]), start=(mc == 0), stop=(mc == 15)),
                        reads=[mt.r(mo_ + 128 * mc, mo_ + 128 * mc + 128),
                               Wo.r(1024 * mc + 512 * dh, 1024 * mc + 512 * dh + 512)], writes=[pt.r()])
                S.add('dve', lambda e, pt=pt, dh=dh, xr_=xr_, ho=ho: e.tensor_tensor(
                    hsb.ap(ho + 512 * dh, [[1, 512]]), pt.ap(0, [[1, 512]]), xr_.ap(512 * dh, [[1, 512]]),
                    ALU.add),
                    reads=[pt.r(), xr_.r(512 * dh, 512 * dh + 512)],
                    writes=[hsb.r(ho + 512 * dh, ho + 512 * dh + 512)])
            sm = nextS()
            S.add('act', lambda e, sm=sm, ho=ho: e.activation(
                hsb.ap(ho + 1024, [[1, 1024]]), hsb.ap(ho, [[1, 1024]]), AF.Square, accum_out=sm.ap(0, [[1, 1]])),
                reads=[hsb.r(ho, ho + 1024)], writes=[hsb.r(ho + 1024, ho + 2048), sm.r(0, 1)])
            S.add('dve', lambda e, sm=sm: e.tensor_scalar(sm.ap(1, [[1, 1]]), sm.ap(0, [[1, 1]]),
                                                          1.0 / D, EPS, ALU.mult, ALU.add),
                  reads=[sm.r(0, 1)], writes=[sm.r(1, 2)])
            S.add('act', lambda e, sm=sm: e.activation(sm.ap(2, [[1, 1]]), sm.ap(1, [[1, 1]]), AF.Sqrt),
                  reads=[sm.r(1, 2)], writes=[sm.r(2, 3)])
            S.add('dve', lambda e, sm=sm: e.reciprocal(sm.ap(3, [[1, 1]]), sm.ap(2, [[1, 1]])),
                  reads=[sm.r(2, 3)], writes=[sm.r(3, 4)])
            S.add('dve', lambda e, sm=sm, ho=ho: e.scalar_tensor_tensor(
                hsb.ap(ho + 1024, [[1, 1024]]), hsb.ap(ho, [[1, 1024]]), sm.ap(3, [[1, 1]]),
                fgb.ap(0, [[1, 1024]]), ALU.mult, ALU.mult),
                reads=[hsb.r(ho, ho + 1024), sm.r(3, 4), fgb.r(0, 1024)], writes=[hsb.r(ho + 1024, ho + 2048)])
            S.add('pool', lambda e, tt=tt, ho=ho: e.dma_start(out=out_d[128 * tt:128 * (tt + 1), :],
                                                              in_=hsb.ap(ho + 1024, [[1, 1024]])),
                  reads=[hsb.r(ho + 1024, ho + 2048)], writes=[("outd", tt, tt + 1)], dma='outw%d' % (tt % 2))

        S.emit(nc, (['outw0', 'outw1'] if '3' in PH else []) + (['dbg'] if dbg else []))
    return nc


def _bf(a):
    return np.ascontiguousarray(a.astype(np.float32)).astype(ml_dtypes.bfloat16)


def _consts():
    N = 8192
    th = 2 * np.pi / N
    a = np.arange(128)[:, None, None].astype(np.float64)
    b = np.arange(32)[None, :, None].astype(np.float64)
    k1 = np.arange(128)[None, None, :].astype(np.float64)
    ang = th * (32 * a + b) * (k1 + 0.5)
    M1 = np.stack([np.cos(ang), -np.sin(ang)], axis=2).reshape(128, 8192)
    bb = np.arange(32)[:, None].astype(np.float64)
    k2 = np.arange(32)[None, :].astype(np.float64)
    ph = 2 * np.pi * bb * k2 / 32
    I4 = np.eye(4)
    C2 = np.kron(np.cos(ph), I4)
    S2 = np.kron(np.sin(ph), I4)
    S2m = np.concatenate([C2, S2, -S2], axis=1)
    IC = np.kron(np.cos(ph).T, I4)
    IS = np.kron(np.sin(ph).T, I4)
    R12 = np.concatenate([IC, IS, -IS, IC], axis=1)
    k1c = np.arange(128)[:, None, None].astype(np.float64)
    bp = np.arange(32)[None, :, None].astype(np.float64)
    ap_ = np.arange(64)[None, None, :].astype(np.float64)
    ang2 = th * (32 * ap_ + bp) * (k1c + 0.5)
    IM = np.stack([(2.0 / N) * np.cos(ang2), -(2.0 / N) * np.sin(ang2)], axis=2).reshape(128, 4096)
    t = np.linspace(0.0, 1.0, L, dtype=np.float32)[:, None]
    bands = 16
    w = (np.float32(2.0 * math.pi / L) * np.arange(L, dtype=np.float32))[:, None]
    f = np.linspace(1e-4, bands - 1, bands, dtype=np.float32)[None, :]
    fw = (f * w).astype(np.float32)
    z = np.concatenate([t, np.cos(fw), -np.sin(fw)], axis=-1).astype(np.float32)
    negt = (-t[:, 0]).reshape(128, 32).astype(np.float32)
    return dict(M1=_bf(M1), S2m=_bf(S2m), R12=_bf(R12), IM=_bf(IM),
                ident=_bf(np.eye(128)), zT=np.ascontiguousarray(z.T), negt=np.ascontiguousarray(negt))


_CACHE = {}


def kernel(x, norm_g, w_in, conf_dw_w, conf_dw_b, conf_ln_g, conf_ln_b,
           hy_short_w, hy_short_b, filt_w1, filt_b1, filt_freq1,
           filt_w2, filt_b2, filt_freq2, filt_w3, filt_b3, filt_freq3,
           filt_w_out, hy_deltas, hy_skip, hy_norm_g, w_out, final_g):
    f32 = np.float32
    x = np.asarray(x, f32)
    if 'nc' not in _CACHE:
        _CACHE['nc'] = build_program()
        _CACHE['c'] = _consts()
    nc = _CACHE['nc']
    cs = _CACHE['c']

    def fm(v):
        return np.asarray(v, f32).reshape(8, 128).T

    common = dict(
        w_in=np.ascontiguousarray(np.asarray(w_in, f32)[0]),
        w_out=np.ascontiguousarray(np.asarray(w_out, f32)[0]),
        mlpw1=np.ascontiguousarray(np.asarray(filt_w1, f32)[0]),
        mlpw2=np.ascontiguousarray(np.asarray(filt_w2, f32)[0]),
        mlpw3=np.ascontiguousarray(np.asarray(filt_w3, f32)[0]),
        mlpp=np.ascontiguousarray(np.stack([np.asarray(v, f32)[0] for v in
                                            (filt_b1, filt_b2, filt_b3, filt_freq1, filt_freq2, filt_freq3)], axis=1)),
        skip=np.ascontiguousarray(np.asarray(hy_skip, f32).reshape(1, 1024)),
        final_g=np.ascontiguousarray(np.asarray(final_g, f32).reshape(1, 1024)),
        zT=cs['zT'], negt=cs['negt'], M1=cs['M1'], S2m=cs['S2m'], R12=cs['R12'], IM=cs['IM'], ident=cs['ident'],
    )
    in_maps = []
    for i in range(8):
        b, hf = i // 2, i % 2
        cw = np.asarray(conf_dw_w, f32)[0]
        sw = np.asarray(hy_short_w, f32)[0]
        wf = np.asarray(filt_w_out, f32)[0]
        dl = np.asarray(hy_deltas, f32)[0]
        xl = x[b]
        if hf == 1:
            xl = xl[::-1]
            cw = cw[::-1]
            sw = sw[::-1]
            wf = np.concatenate([wf[:, 1024:], wf[:, :1024]], axis=1)
            dl = dl[::-1]
        pvv = np.zeros((128, NPV), f32)
        pvv[:, 0:8] = fm(np.asarray(norm_g, f32)[0])
        pvv[:, 8:256] = cw.reshape(31, 8, 128).transpose(2, 1, 0).reshape(128, 248)
        pvv[:, 256:264] = fm(np.asarray(conf_dw_b, f32)[0])
        pvv[:, 264:272] = fm(np.asarray(conf_ln_g, f32)[0])
        pvv[:, 272:280] = fm(np.asarray(conf_ln_b, f32)[0])
        pvv[:, 280:352] = sw.reshape(3, 24, 128).transpose(2, 1, 0).reshape(128, 72)
        pvv[:, 352:376] = np.asarray(hy_short_b, f32)[0].reshape(24, 128).T
        pvv[:, 376:384] = fm(np.asarray(hy_norm_g, f32)[0])
        m = dict(common)
        m.update(x=np.ascontiguousarray(xl), pv=pvv, wf=np.ascontiguousarray(wf),
                 deltas=np.ascontiguousarray(dl))
        in_maps.append(m)
    res = run_bass_kernel_spmd(nc, in_maps, core_ids=list(range(8)))
    out = np.empty((4, L, D), f32)
    for i in range(8):
        b, hf = i // 2, i % 2
        o = np.asarray(res.results[i]["out"], f32)
        if hf == 0:
            out[b, :2048] = o
        else:
            out[b, 2048:] = o[::-1]
    return out
```
